# Optimizing a Trainium2 kernel written in Bass

```python
import math
import jax, jax.numpy as jnp
from jax import lax
import numpy as np

D_MODEL = 1024
BATCH = 2
SEQ = 8192
DEPTH = 4

N_A_LAYERS = DEPTH // 2
N_B_LAYERS = DEPTH - N_A_LAYERS
D_RNN = D_MODEL
LRU_HEADS = 4
LRU_BLOCK = D_RNN // LRU_HEADS
CONV_WIDTH = 4
LRU_C = 8.0
MLA_HEADS = 16
QK_NOPE = 64
QK_ROPE = 32
V_HEAD = 64
Q_LORA = 384
KV_LORA = 256
ROPE_THETA = 10000.0
Q_BLOCK = 128
ATTN_SCALE = 1.0 / math.sqrt(QK_NOPE + QK_ROPE)
PEER_HEADS = 8
N_KEYS = 128
N_EXPERTS = N_KEYS * N_KEYS
PEER_TOPK = 16
PEER_QDIM = 256
PEER_HALF = PEER_QDIM // 2
PEER_CHUNK = 128
RMS_EPS = 1e-6
NEG_INF = -1e30

kernel_name = 'yoco_hawk_mla_peer'


def rmsnorm(x, g):
    xf = x.astype(jnp.float32)
    y = xf * lax.rsqrt(jnp.mean(xf * xf, axis=-1, keepdims=True) + RMS_EPS)
    return (y * g.astype(jnp.float32)).astype(x.dtype)


def rope(x, positions):
    half = QK_ROPE // 2
    freqs = ROPE_THETA ** (-jnp.arange(half, dtype=jnp.float32) / half)
    ang = positions.astype(jnp.float32)[..., None] * freqs
    ang = ang.reshape(ang.shape[:2] + (1,) * (x.ndim - 3) + (half,))
    cos, sin = jnp.cos(ang), jnp.sin(ang)
    xf = x.astype(jnp.float32)
    x1, x2 = xf[..., :half], xf[..., half:]
    out = jnp.concatenate([x1 * cos - x2 * sin, x1 * sin + x2 * cos], axis=-1)
    return out.astype(x.dtype)


def _lru_combine(left, right):
    a1, b1 = left
    a2, b2 = right
    return a1 * a2, a2 * b1 + b2


def rglru_block(h, w_in, conv_w, conv_b, wa, ba, wx, bx, lam, w_out):
    B, S, _ = h.shape
    gate_in, rec_in = jnp.split(h @ w_in, 2, axis=-1)
    gate = jax.nn.gelu(gate_in, approximate=False)
    xp = jnp.pad(rec_in, ((0, 0), (CONV_WIDTH - 1, 0), (0, 0)))
    xc = conv_b
    for k in range(CONV_WIDTH):
        xc = xc + xp[:, k:k + S] * conv_w[k]
    xb = xc.reshape(B, S, LRU_HEADS, LRU_BLOCK)
    r = jax.nn.sigmoid(jnp.einsum('bshi,hij->bshj', xb, wa).reshape(B, S, D_RNN) + ba)
    i = jax.nn.sigmoid(jnp.einsum('bshi,hij->bshj', xb, wx).reshape(B, S, D_RNN) + bx)
    log_a = -LRU_C * r.astype(jnp.float32) * jax.nn.softplus(-lam.astype(jnp.float32))
    a = jnp.exp(log_a)
    b = jnp.sqrt(-jnp.expm1(2.0 * log_a)) * (i * xc).astype(jnp.float32)
    _, hs = lax.associative_scan(_lru_combine, (a, b), axis=1)
    return (gate * hs.astype(h.dtype)) @ w_out


def shared_kv(x, c, kv_ada_w, kv_ada_b, kv_norm_g, w_dkv, w_kr, kv_latent_g, w_uk, w_uv, positions):
    B, S, _ = x.shape
    shift, scale = jnp.split((jax.nn.silu(c) @ kv_ada_w + kv_ada_b)[:, None, :], 2, axis=-1)
    h = rmsnorm(x, kv_norm_g) * (1.0 + scale) + shift
    c_kv = rmsnorm(h @ w_dkv, kv_latent_g)
    k_rope = rope(h @ w_kr, positions)
    k_nope = (c_kv @ w_uk).reshape(B, S, MLA_HEADS, QK_NOPE)
    v = (c_kv @ w_uv).reshape(B, S, MLA_HEADS, V_HEAD)
    return k_nope, k_rope, v


def _to_blocks(t, block):
    B, S = t.shape[:2]
    return jnp.moveaxis(t.reshape((B, S // block, block) + t.shape[2:]), 1, 0)


def _from_blocks(t):
    nb, B, blk = t.shape[:3]
    return jnp.moveaxis(t, 0, 1).reshape((B, nb * blk) + t.shape[3:])


def mla_block(h, w_dq, q_latent_g, w_uq, w_o, k_nope, k_rope, v, positions):
    B, S, _ = h.shape
    q_lat = rmsnorm(h @ w_dq, q_latent_g)
    q = (q_lat @ w_uq).reshape(B, S, MLA_HEADS, QK_NOPE + QK_ROPE)
    q_nope = q[..., :QK_NOPE]
    q_rope = rope(q[..., QK_NOPE:], positions)

    def attend(args):
        qn, qr, qpos = args
        s = (jnp.einsum('bqhd,bkhd->bhqk', qn, k_nope)
             + jnp.einsum('bqhr,bkr->bhqk', qr, k_rope)).astype(jnp.float32) * ATTN_SCALE
        mask = positions[:, None, None, :] <= qpos[:, None, :, None]
        p = jax.nn.softmax(jnp.where(mask, s, NEG_INF), axis=-1).astype(v.dtype)
        return jnp.einsum('bhqk,bkhd->bqhd', p, v)

    out = lax.map(attend, (_to_blocks(q_nope, Q_BLOCK), _to_blocks(q_rope, Q_BLOCK),
                           _to_blocks(positions, Q_BLOCK)))
    out = _from_blocks(out).reshape(B, S, MLA_HEADS * V_HEAD)
    return out @ w_o


def peer(h, w_q, sub_keys, u_tab, v_tab):
    B, S, _ = h.shape
    q = (h @ w_q).reshape(B, S, PEER_HEADS, 2, PEER_HALF)
    s = jnp.einsum('bshpd,hpnd->bshpn', q, sub_keys).astype(jnp.float32)
    v1, i1 = lax.top_k(s[..., 0, :], PEER_TOPK)
    v2, i2 = lax.top_k(s[..., 1, :], PEER_TOPK)
    cand = (v1[..., :, None] + v2[..., None, :]).reshape(B, S, PEER_HEADS, PEER_TOPK * PEER_TOPK)
    cv, ci = lax.top_k(cand, PEER_TOPK)
    e1 = jnp.take_along_axis(i1, ci // PEER_TOPK, axis=-1)
    e2 = jnp.take_along_axis(i2, ci % PEER_TOPK, axis=-1)
    idx = e1 * N_KEYS + e2
    g = jax.nn.softmax(cv, axis=-1).astype(h.dtype)

    def experts(args):
        hc, ic, gc = args
        act = jax.nn.gelu(jnp.einsum('bcd,bchkd->bchk', hc, u_tab[ic]), approximate=False)
        return jnp.einsum('bchk,bchkd->bcd', gc * act, v_tab[ic])

    out = lax.map(experts, (_to_blocks(h, PEER_CHUNK), _to_blocks(idx, PEER_CHUNK),
                            _to_blocks(g, PEER_CHUNK)))
    return _from_blocks(out)


def _normal(key, shape, fan_in):
    return jax.random.normal(key, shape, jnp.float32) * (fan_in ** -0.5)


def _gain(key, shape):
    return 1.0 + 0.02 * jax.random.normal(key, shape, jnp.float32)


def _bias(key, shape):
    return 0.02 * jax.random.normal(key, shape, jnp.float32)


def setup_inputs(seed: int = 0) -> dict:
    key = jax.random.key(seed)
    ks = iter(jax.random.split(key, 40))
    D = D_MODEL
    x = jax.random.normal(next(ks), (BATCH, SEQ, D), jnp.float32)
    c = jax.random.normal(next(ks), (BATCH, D), jnp.float32)
    offset = jax.random.randint(next(ks), (BATCH, 1), 0, 1024, dtype=jnp.int32)
    positions = offset + jnp.arange(SEQ, dtype=jnp.int32)[None, :]
    u = jax.random.uniform(next(ks), (N_A_LAYERS, D_RNN), jnp.float32, 0.9, 0.999)
    a0 = u ** (1.0 / LRU_C)
    lru_lambda = jnp.log(a0) - jnp.log1p(-a0)
    return {
        'x': x, 'c': c, 'positions': positions,
        'ada_w': _normal(next(ks), (DEPTH, D, 6 * D), D),
        'ada_b': _bias(next(ks), (DEPTH, 6 * D)),
        'norm_mix_g': _gain(next(ks), (DEPTH, D)),
        'norm_ffn_g': _gain(next(ks), (DEPTH, D)),
        'lru_w_in': _normal(next(ks), (N_A_LAYERS, D, 2 * D_RNN), D),
        'lru_conv_w': _normal(next(ks), (N_A_LAYERS, CONV_WIDTH, D_RNN), CONV_WIDTH),
        'lru_conv_b': _bias(next(ks), (N_A_LAYERS, D_RNN)),
        'lru_wa': _normal(next(ks), (N_A_LAYERS, LRU_HEADS, LRU_BLOCK, LRU_BLOCK), LRU_BLOCK),
        'lru_ba': _bias(next(ks), (N_A_LAYERS, D_RNN)),
        'lru_wx': _normal(next(ks), (N_A_LAYERS, LRU_HEADS, LRU_BLOCK, LRU_BLOCK), LRU_BLOCK),
        'lru_bx': _bias(next(ks), (N_A_LAYERS, D_RNN)),
        'lru_lambda': lru_lambda,
        'lru_w_out': _normal(next(ks), (N_A_LAYERS, D_RNN, D), D_RNN),
        'kv_ada_w': _normal(next(ks), (D, 2 * D), D),
        'kv_ada_b': _bias(next(ks), (2 * D,)),
        'kv_norm_g': _gain(next(ks), (D,)),
        'mla_w_dkv': _normal(next(ks), (D, KV_LORA), D),
        'mla_w_kr': _normal(next(ks), (D, QK_ROPE), D),
        'mla_kv_latent_g': _gain(next(ks), (KV_LORA,)),
        'mla_w_uk': _normal(next(ks), (KV_LORA, MLA_HEADS * QK_NOPE), KV_LORA),
        'mla_w_uv': _normal(next(ks), (KV_LORA, MLA_HEADS * V_HEAD), KV_LORA),
        'mla_w_dq': _normal(next(ks), (N_B_LAYERS, D, Q_LORA), D),
        'mla_q_latent_g': _gain(next(ks), (N_B_LAYERS, Q_LORA)),
        'mla_w_uq': _normal(next(ks), (N_B_LAYERS, Q_LORA, MLA_HEADS * (QK_NOPE + QK_ROPE)), Q_LORA),
        'mla_w_o': _normal(next(ks), (N_B_LAYERS, MLA_HEADS * V_HEAD, D), MLA_HEADS * V_HEAD),
        'peer_w_q': _normal(next(ks), (DEPTH, D, PEER_HEADS * PEER_QDIM), D),
        'peer_sub_keys': _normal(next(ks), (DEPTH, PEER_HEADS, 2, N_KEYS, PEER_HALF), PEER_HALF),
        'peer_u': _normal(next(ks), (DEPTH, N_EXPERTS, D), D),
        'peer_v': _normal(next(ks), (DEPTH, N_EXPERTS, D), PEER_HEADS),
        'final_g': _gain(next(ks), (D,)),
    }


def reference(x, c, positions, ada_w, ada_b, norm_mix_g, norm_ffn_g,
              lru_w_in, lru_conv_w, lru_conv_b, lru_wa, lru_ba, lru_wx, lru_bx, lru_lambda, lru_w_out,
              kv_ada_w, kv_ada_b, kv_norm_g, mla_w_dkv, mla_w_kr, mla_kv_latent_g, mla_w_uk, mla_w_uv,
              mla_w_dq, mla_q_latent_g, mla_w_uq, mla_w_o,
              peer_w_q, peer_sub_keys, peer_u, peer_v, final_g):
    k_nope = k_rope = v = None
    for l in range(DEPTH):
        if l == N_A_LAYERS:
            k_nope, k_rope, v = shared_kv(x, c, kv_ada_w, kv_ada_b, kv_norm_g, mla_w_dkv, mla_w_kr,
                                          mla_kv_latent_g, mla_w_uk, mla_w_uv, positions)
        mod = (jax.nn.silu(c) @ ada_w[l] + ada_b[l])[:, None, :]
        sh1, sc1, g1, sh2, sc2, g2 = jnp.split(mod, 6, axis=-1)
        h = rmsnorm(x, norm_mix_g[l]) * (1.0 + sc1) + sh1
        if l < N_A_LAYERS:
            y = rglru_block(h, lru_w_in[l], lru_conv_w[l], lru_conv_b[l], lru_wa[l], lru_ba[l],
                            lru_wx[l], lru_bx[l], lru_lambda[l], lru_w_out[l])
        else:
            j = l - N_A_LAYERS
            y = mla_block(h, mla_w_dq[j], mla_q_latent_g[j], mla_w_uq[j], mla_w_o[j],
                          k_nope, k_rope, v, positions)
        x = x + g1 * y
        h = rmsnorm(x, norm_ffn_g[l]) * (1.0 + sc2) + sh2
        x = x + g2 * peer(h, peer_w_q[l], peer_sub_keys[l], peer_u[l], peer_v[l])
    return rmsnorm(x, final_g)
```

```python
import numpy as np
from contextlib import ExitStack
import concourse.bass as bass
import concourse.mybir as mybir
from concourse.bass_utils import run_bass_kernel_spmd

F32 = mybir.dt.float32
BF16 = mybir.dt.bfloat16
I32 = mybir.dt.int32
U32 = mybir.dt.uint32
ALU = mybir.AluOpType
AF = mybir.ActivationFunctionType
AX = mybir.AxisListType

N_DMA_SEMS = 24


class Buf:
    __slots__ = ("name", "w", "r")

    def __init__(self, name):
        self.name = name
        self.w = None
        self.r = {}


class Tile:
    def __init__(self, t, name):
        self.t = t
        self.buf = Buf(name)

    def __getitem__(self, k):
        return self.t[k]


class Prog:
    def __init__(self, nc):
        self.nc = nc
        self.es = ExitStack()
        self.eng = {"pe": nc.tensor, "dve": nc.vector, "act": nc.scalar,
                    "pool": nc.gpsimd, "sp": nc.sync}
        self.sem = {k: self.es.enter_context(nc.semaphore("s_" + k)) for k in self.eng}
        self.cnt = {k: 0 for k in self.eng}
        self.seen = {k: {} for k in self.eng}
        self.dsem = [self.es.enter_context(nc.semaphore("d%d" % i)) for i in range(N_DMA_SEMS)]
        self.dcnt = [0] * N_DMA_SEMS
        self.dnext = 0
        self.nins = 0
        self.out_events = []

    def sb(self, name, shape, dt):
        t = self.es.enter_context(self.nc.sbuf_tensor(name, list(shape), dt))
        return Tile(t, name)

    def ps(self, name, shape, dt=F32):
        t = self.es.enter_context(self.nc.psum_tensor(name, list(shape), dt))
        return Tile(t, name)

    def _semh(self, key):
        return self.sem[key] if isinstance(key, str) else self.dsem[key[1]]

    def _wait(self, e, key, val):
        if self.seen[e].get(key, 0) >= val:
            return
        self.eng[e].wait_ge(self._semh(key), val)
        self.seen[e][key] = val

    def _deps(self, e, reads, writes):
        for b in reads:
            b = b.buf if isinstance(b, Tile) else b
            if b.w is not None:
                k, v = b.w
                if k == e and e == "pe":
                    continue
                self._wait(e, k, v)
        for b in writes:
            b = b.buf if isinstance(b, Tile) else b
            if b.w is not None:
                k, v = b.w
                if k != e:
                    self._wait(e, k, v)
            for k, v in b.r.items():
                if k != e:
                    self._wait(e, k, v)

    def _mark(self, ev, reads, writes):
        for b in reads:
            b = b.buf if isinstance(b, Tile) else b
            k, v = ev
            if b.r.get(k, 0) < v:
                b.r[k] = v
        for b in writes:
            b = b.buf if isinstance(b, Tile) else b
            b.w = ev
            b.r = {}

    def op(self, e, fn, reads=(), writes=()):
        self._deps(e, reads, writes)
        ins = fn(self.eng[e])
        self.cnt[e] += 1
        ins.then_inc(self.sem[e], 1)
        self._mark((e, self.cnt[e]), reads, writes)
        self.nins += 1
        return ins

    def dma(self, e, out, in_, reads=(), writes=(), is_output=False, **kw):
        i = self.dnext
        self.dnext = (self.dnext + 1) % N_DMA_SEMS
        key = ("dma", i)
        if self.dcnt[i] > 0:
            self._wait(e, key, self.dcnt[i])
        self._deps(e, reads, writes)
        ins = self.eng[e].dma_start(out=out, in_=in_, **kw)
        self.dcnt[i] += 16
        ins.then_inc(self.dsem[i], 16)
        ev = (key, self.dcnt[i])
        self._mark(ev, reads, writes)
        if is_output:
            self.out_events.append(ev)
        self.nins += 1
        return ins

    def finish(self):
        for k, v in self.out_events:
            self._wait("sp", k, v)
        for i in range(N_DMA_SEMS):
            if self.dcnt[i] > 0:
                self._wait("sp", ("dma", i), self.dcnt[i])
        self.es.close()


D = 1024
KC = 8
T = 2048
BLK = 512
NBLK = T // BLK
SEQ = 8192
EPS = 1e-6


def ext_in(nc, name, shape, dt=F32):
    return nc.dram_tensor(name, list(shape), dt, kind="ExternalInput").ap()


def ext_out(nc, name, shape, dt=F32):
    return nc.dram_tensor(name, list(shape), dt, kind="ExternalOutput").ap()


class Common:
    def __init__(self, p, need_ident=False, need_stage=True):
        self.p = p
        self.ones = p.sb("ones_bf", [128, 128], BF16)
        p.op("pool", lambda e: e.memset(self.ones[:], 1.0), writes=[self.ones])
        self.stage = [p.sb("stage%d" % i, [128, 2048], F32) for i in range(2)] if need_stage else []
        self.sti = 0
        if need_ident:
            io = p.sb("io", [128, 128], F32)
            pid = p.sb("pid", [128, 1], F32)
            self.identf = p.sb("identf", [128, 128], F32)
            self.ident = p.sb("ident", [128, 128], BF16)
            p.op("pool", lambda e: e.iota(io[:], [[1, 128]], base=0, channel_multiplier=0,
                                          allow_small_or_imprecise_dtypes=True), writes=[io])
            p.op("pool", lambda e: e.iota(pid[:], [[0, 1]], base=0, channel_multiplier=1,
                                          allow_small_or_imprecise_dtypes=True), writes=[pid])
            p.op("dve", lambda e: e.tensor_scalar(self.identf[:], io[:], pid[:, 0:1], None, ALU.is_equal),
                 reads=[io, pid], writes=[self.identf])
            p.op("dve", lambda e: e.tensor_copy(self.ident[:], self.identf[:]),
                 reads=[self.identf], writes=[self.ident])

    def next_stage(self):
        s = self.stage[self.sti % 2]
        self.sti += 1
        return s


def load_cast(p, cm, dram2d, dst_ap_fn, nk, M, cast_eng="pool", dst=None):
    for k in range(nk):
        st = cm.next_stage()
        p.dma("sp", st[:, 0:M], dram2d[k * 128:(k + 1) * 128, :], writes=[st])
        p.op(cast_eng, lambda e: e.tensor_copy(dst_ap_fn(k), st[:, 0:M]), reads=[st], writes=[dst])


def small_in(p, name, dram, shape, dt=F32):
    t = p.sb(name, shape, dt)
    p.dma("sp", t[:], dram, writes=[t])
    return t


def norm_block(p, cm, xin, nk, n, Acol, Bcol, out_fn, outbuf, wk, divisor, ps, reads_extra=()):
    sq, srt, rstd, tmp = wk["sq"], wk["srt"], wk["rstd"], wk["tmp"]
    p.op("act", lambda e: e.activation(sq[:, 0:nk, 0:n], xin[:, 0:nk, 0:n], AF.Square),
         reads=[xin], writes=[sq])
    for k in range(nk):
        p.op("pe", lambda e: e.matmul(ps[:, 0:n], cm.ones[:], sq[:, k, 0:n], start=(k == 0), stop=(k == nk - 1)),
             reads=[sq, cm.ones], writes=[ps])
    p.op("act", lambda e: e.activation(srt[:, 0:n], ps[:, 0:n], AF.Sqrt, scale=1.0 / divisor, bias=wk["eps"][:, 0:1]),
         reads=[ps, wk["eps"]], writes=[srt])
    p.op("dve", lambda e: e.reciprocal(rstd[:, 0:n], srt[:, 0:n]), reads=[srt], writes=[rstd])
    p.op("dve", lambda e: e.tensor_tensor(tmp[:, 0:nk, 0:n], xin[:, 0:nk, 0:n],
                                          rstd[:, 0:n].unsqueeze(1).to_broadcast([128, nk, n]), ALU.mult),
         reads=[xin, rstd], writes=[tmp])
    for k in range(nk):
        eng = "act" if k % 2 == 0 else "pool"
        if eng == "act":
            p.op("act", lambda e: e.activation(out_fn(k), tmp[:, k, 0:n], AF.Identity,
                                               scale=Acol[:, k:k + 1], bias=Bcol[:, k:k + 1]),
                 reads=[tmp, Acol, Bcol], writes=[outbuf])
        else:
            p.op("pool", lambda e: e.tensor_scalar(out_fn(k), tmp[:, k, 0:n], Acol[:, k:k + 1], Bcol[:, k:k + 1],
                                                   ALU.mult, ALU.add),
                 reads=[tmp, Acol, Bcol], writes=[outbuf])


def norm_work(p, nk=KC, n=BLK):
    wk = {
        "sq": p.sb("n_sq", [128, nk, n], BF16),
        "srt": p.sb("n_srt", [128, n], F32),
        "rstd": p.sb("n_rstd", [128, n], F32),
        "tmp": p.sb("n_tmp", [128, nk, n], F32),
        "eps": p.sb("n_eps", [128, 1], F32),
    }
    p.op("pool", lambda e: e.memset(wk["eps"][:], EPS), writes=[wk["eps"]])
    return wk


def mod_cols(p, name, modt, ng, sh_i, sc_i):
    A = p.sb(name + "_A", [128, KC], F32)
    t = p.sb(name + "_t", [128, KC], F32)
    p.op("dve", lambda e: e.tensor_scalar(t[:], modt[:, sc_i * 8:sc_i * 8 + 8], 1.0, None, ALU.add),
         reads=[modt], writes=[t])
    p.op("dve", lambda e: e.tensor_tensor(A[:], t[:], ng[:], ALU.mult), reads=[t, ng], writes=[A])
    return A


def build_mod():
    nc = bass.Bass("TRN2", target_bir_lowering=False)
    W = ext_in(nc, "W", [D, 6144])
    cT = ext_in(nc, "cT", [128, KC])
    bT = ext_in(nc, "bT", [128, 48])
    out = ext_out(nc, "out", [128, 48])
    p = Prog(nc)
    c_sb = small_in(p, "c_sb", cT, [128, KC])
    b_sb = small_in(p, "b_sb", bT, [128, 48])
    sil = p.sb("sil", [128, KC], F32)
    p.op("act", lambda e: e.activation(sil[:], c_sb[:], AF.Silu), reads=[c_sb], writes=[sil])
    wt = [p.sb("wt%d" % i, [128, KC, 1024], F32) for i in range(2)]
    ps = p.ps("ps", [128, 512], F32)
    o_sb = p.sb("o_sb", [128, 48], F32)
    for g in range(6):
        w = wt[g % 2]
        for k in range(KC):
            p.dma("sp", w[:, k, :], W[k * 128:(k + 1) * 128, g * 1024:(g + 1) * 1024], writes=[w])
        for m in range(8):
            col = g * 8 + m
            for k in range(KC):
                p.op("pe", lambda e: e.matmul(ps[:, col:col + 1], w[:, k, m * 128:(m + 1) * 128], sil[:, k:k + 1],
                                              start=(k == 0), stop=(k == KC - 1)),
                     reads=[w, sil], writes=[ps])
    p.op("dve", lambda e: e.tensor_tensor(o_sb[:], ps[:, 0:48], b_sb[:], ALU.add), reads=[ps, b_sb], writes=[o_sb])
    p.dma("sp", out, o_sb[:], reads=[o_sb], is_output=True)
    p.finish()
    return nc


NE_CONV = 2048 * 4


def build_conv():
    nc = bass.Bass("TRN2", target_bir_lowering=False)
    U = ext_in(nc, "U", [NE_CONV, D])
    V = ext_in(nc, "V", [NE_CONV, D])
    uT = ext_out(nc, "uT", [128, KC, NE_CONV], BF16)
    vb = ext_out(nc, "vb", [NE_CONV, D], BF16)
    p = Prog(nc)
    cm = Common(p, need_ident=True)
    nch = NE_CONV // 128
    uin = [p.sb("uin%d" % i, [128, D], F32) for i in range(2)]
    vin = [p.sb("vin%d" % i, [128, D], F32) for i in range(2)]
    vo = [p.sb("vo%d" % i, [128, D], BF16) for i in range(2)]
    uo = [p.sb("uo%d" % i, [128, KC, 512], BF16) for i in range(2)]
    pst = [p.ps("pst%d" % i, [128, 4, 128], F32) for i in range(4)]
    for ch in range(nch):
        ui = uin[ch % 2]
        vi = vin[ch % 2]
        p.dma("sp", ui[:], U[ch * 128:(ch + 1) * 128, :], writes=[ui])
        p.dma("sp", vi[:], V[ch * 128:(ch + 1) * 128, :], writes=[vi])
        v_o = vo[ch % 2]
        p.op("pool", lambda e: e.tensor_copy(v_o[:], vi[:]), reads=[vi], writes=[v_o])
        p.dma("pool", vb[ch * 128:(ch + 1) * 128, :], v_o[:], reads=[v_o], is_output=True)
        u_o = uo[(ch // 4) % 2]
        sub = ch % 4
        for half in range(2):
            pt = pst[(ch * 2 + half) % 4]
            for kk in range(4):
                k = half * 4 + kk
                p.op("pe", lambda e: e.transpose(pt[:, kk, :], ui[:, k * 128:(k + 1) * 128], cm.identf[:]),
                     reads=[ui, cm.identf], writes=[pt])
            eng = "act" if half == 0 else "dve"
            if eng == "act":
                p.op("act", lambda e: e.activation(u_o[:, half * 4:half * 4 + 4, sub * 128:(sub + 1) * 128], pt[:], AF.Copy),
                     reads=[pt], writes=[u_o])
            else:
                p.op("dve", lambda e: e.tensor_copy(u_o[:, half * 4:half * 4 + 4, sub * 128:(sub + 1) * 128], pt[:]),
                     reads=[pt], writes=[u_o])
        if sub == 3:
            g = ch // 4
            p.dma("sp", uT[:, :, g * 512:(g + 1) * 512], u_o[:], reads=[u_o], is_output=True)
    p.finish()
    return nc


def build_lru():
    nc = bass.Bass("TRN2", target_bir_lowering=False)
    xT = ext_in(nc, "xT", [128, KC, T])
    xh = ext_in(nc, "xh", [128, KC, 4])
    flag = ext_in(nc, "flag", [128, 1])
    modd = ext_in(nc, "mod", [128, 24])
    ngd = ext_in(nc, "ng", [128, KC])
    w_in = ext_in(nc, "w_in", [D, 2 * D])
    cwd = ext_in(nc, "cw", [128, KC, 4])
    cbd = ext_in(nc, "cb", [128, KC])
    wad = ext_in(nc, "wa", [4, 256, 256])
    bad = ext_in(nc, "ba", [128, KC])
    wxd = ext_in(nc, "wx", [4, 256, 256])
    bxd = ext_in(nc, "bx", [128, KC])
    lamd = ext_in(nc, "lam", [128, KC])
    w_out = ext_in(nc, "w_out", [D, D])
    sumA = ext_in(nc, "sumA", [128, 4, KC])
    sumB = ext_in(nc, "sumB", [128, 4, KC])
    cmaskd = ext_in(nc, "cmask", [128, 4])
    xo = ext_out(nc, "xo", [128, KC, T])
    sAo = ext_out(nc, "sA", [128, KC])
    sBo = ext_out(nc, "sB", [128, KC])

    p = Prog(nc)
    cm = Common(p)
    wk = norm_work(p)
    flg = small_in(p, "flg", flag, [128, 1])
    modt = small_in(p, "modt", modd, [128, 24])
    ng = small_in(p, "ngt", ngd, [128, KC])
    cw = small_in(p, "cwt", cwd, [128, KC, 4])
    cb = small_in(p, "cbt", cbd, [128, KC])
    ba = small_in(p, "bat", bad, [128, KC])
    bx = small_in(p, "bxt", bxd, [128, KC])
    lam = small_in(p, "lamt", lamd, [128, KC])
    sA_in = small_in(p, "sA_in", sumA, [128, 4, KC])
    sB_in = small_in(p, "sB_in", sumB, [128, 4, KC])
    cmask = small_in(p, "cmaskt", cmaskd, [128, 4])

    A1 = mod_cols(p, "m1", modt, ng, 0, 1)
    e1 = p.sb("e1", [128, KC], F32)
    scol = p.sb("scol", [128, KC], F32)
    scol2 = p.sb("scol2", [128, KC], F32)
    p.op("act", lambda e: e.activation(e1[:], lam[:], AF.Exp, scale=-1.0), reads=[lam], writes=[e1])
    p.op("dve", lambda e: e.tensor_scalar(e1[:], e1[:], 1.0, None, ALU.add), reads=[e1], writes=[e1])
    p.op("act", lambda e: e.activation(e1[:], e1[:], AF.Ln), reads=[e1], writes=[e1])
    p.op("dve", lambda e: e.tensor_scalar(scol[:], e1[:], -8.0, None, ALU.mult), reads=[e1], writes=[scol])
    p.op("dve", lambda e: e.tensor_scalar(scol2[:], e1[:], -16.0, None, ALU.mult), reads=[e1], writes=[scol2])

    carry = p.sb("carry", [128, KC], F32)
    cprod = p.sb("cprod", [128, KC], F32)
    t1 = p.sb("t1", [128, KC], F32)
    p.op("dve", lambda e: e.memset(carry[:], 0.0), writes=[carry])
    p.op("dve", lambda e: e.memset(cprod[:], 1.0), writes=[cprod])
    for j in range(4):
        p.op("dve", lambda e: e.tensor_tensor(t1[:], sA_in[:, j, :], carry[:], ALU.mult), reads=[sA_in, carry], writes=[t1])
        p.op("dve", lambda e: e.tensor_tensor(t1[:], t1[:], sB_in[:, j, :], ALU.add), reads=[t1, sB_in], writes=[t1])
        p.op("dve", lambda e: e.tensor_tensor(t1[:], t1[:], carry[:], ALU.subtract), reads=[t1, carry], writes=[t1])
        p.op("dve", lambda e: e.scalar_tensor_tensor(carry[:], t1[:], cmask[:, j:j + 1], carry[:], ALU.mult, ALU.add),
             reads=[t1, cmask, carry], writes=[carry])

    win = p.sb("win", [128, KC, 2 * D], BF16)
    wout = p.sb("wout", [128, KC, D], BF16)
    wab = p.sb("wab", [128, 4, 2, 256], BF16)
    wxb = p.sb("wxb", [128, 4, 2, 256], BF16)
    load_cast(p, cm, w_in, lambda k: win[:, k, :], KC, 2 * D, dst=win)
    load_cast(p, cm, w_out, lambda k: wout[:, k, :], KC, D, dst=wout)
    for (src, dstt) in ((wad, wab), (wxd, wxb)):
        st = cm.next_stage()
        p.dma("sp", st[:, 0:2048].rearrange("p (h kk m) -> p h kk m", h=4, kk=2),
              src.rearrange("h (kk p) m -> p h kk m", p=128), writes=[st])
        p.op("pool", lambda e: e.tensor_copy(dstt[:].rearrange("p h kk m -> p (h kk m)"), st[:, 0:2048]),
             reads=[st], writes=[dstt])

    hT = p.sb("hT", [128, KC, BLK], BF16)
    yT = p.sb("yT", [128, KC, BLK], BF16)
    xblk = [p.sb("xblk%d" % i, [128, KC, BLK], F32) for i in range(2)]
    psn = p.ps("psn", [128, 512], F32)
    pmm = [p.ps("pmm%d" % i, [128, 512], F32) for i in range(4)]
    pmi = [0]

    def next_ps():
        t = pmm[pmi[0] % 4]
        pmi[0] += 1
        return t

    xhh = p.sb("xhh", [128, KC, 4], F32)
    hh = p.sb("hh", [128, KC, 4], BF16)
    tail = p.sb("tail", [128, KC, 4], F32)
    p.dma("sp", xhh[:], xh, writes=[xhh])
    norm_block(p, cm, xhh, KC, 4, A1, modt, lambda k: hh[:, k, :], hh, wk, float(D), psn)
    for j in range(KC):
        c0 = D + j * 128
        ps = next_ps()
        for k in range(KC):
            p.op("pe", lambda e: e.matmul(ps[:, 0:4], win[:, k, c0:c0 + 128], hh[:, k, :], start=(k == 0), stop=(k == KC - 1)),
                 reads=[win, hh], writes=[ps])
        p.op("dve", lambda e: e.tensor_scalar(tail[:, j, :], ps[:, 0:4], flg[:, 0:1], None, ALU.mult),
             reads=[ps, flg], writes=[tail])

    rec = p.sb("rec", [128, 2, 4 + BLK], F32)
    xc = p.sb("xc", [128, 2, BLK], F32)
    xcb = p.sb("xcb", [128, 2, BLK], BF16)
    rr = p.sb("rr", [128, 2, BLK], F32)
    ii = p.sb("ii", [128, 2, BLK], F32)
    aa = p.sb("aa", [128, 2, BLK], F32)
    a2 = p.sb("a2", [128, 2, BLK], F32)
    hs = p.sb("hs", [128, 2, BLK], F32)
    pscan = p.sb("pscan", [128, BLK], F32)
    zeros = p.sb("zeros", [128, BLK], F32)
    gt = p.sb("gt", [128, 2, BLK], F32)
    p.op("pool", lambda e: e.memset(zeros[:], 0.0), writes=[zeros])

    for b in range(NBLK):
        xb = xblk[b % 2]
        p.dma("sp", xb[:], xT[:, :, b * BLK:(b + 1) * BLK], writes=[xb])
        norm_block(p, cm, xb, KC, BLK, A1, modt, lambda k: hT[:, k, :], hT, wk, float(D), psn)
        for hd in range(4):
            for jj in range(2):
                j = 2 * hd + jj
                c0 = D + j * 128
                p.op("pool", lambda e: e.tensor_copy(rec[:, jj, 0:4], tail[:, j, :]), reads=[tail], writes=[rec])
                ps = next_ps()
                for k in range(KC):
                    p.op("pe", lambda e: e.matmul(ps[:], win[:, k, c0:c0 + 128], hT[:, k, :],
                                                  start=(k == 0), stop=(k == KC - 1)),
                         reads=[win, hT], writes=[ps])
                p.op("act", lambda e: e.activation(rec[:, jj, 4:4 + BLK], ps[:], AF.Copy),
                     reads=[ps], writes=[rec])
                p.op("pool", lambda e: e.tensor_copy(tail[:, j, :], rec[:, jj, BLK:BLK + 4]), reads=[rec], writes=[tail])
            o = 4
            for jj in range(2):
                j = 2 * hd + jj
                p.op("dve", lambda e: e.tensor_scalar(xc[:, jj, :], rec[:, jj, o:o + BLK], cw[:, j, 3:4], cb[:, j:j + 1],
                                                      ALU.mult, ALU.add), reads=[rec, cw, cb], writes=[xc])
                for s in range(1, 4):
                    p.op("dve", lambda e: e.scalar_tensor_tensor(xc[:, jj, :], rec[:, jj, o - s:o - s + BLK],
                                                                 cw[:, j, 3 - s:4 - s], xc[:, jj, :], ALU.mult, ALU.add),
                         reads=[rec, cw, xc], writes=[xc])
            p.op("act", lambda e: e.activation(xcb[:], xc[:], AF.Copy), reads=[xc], writes=[xcb])
            for (wg, bg, dstg) in ((wab, ba, rr), (wxb, bx, ii)):
                for jj in range(2):
                    j = 2 * hd + jj
                    ps = next_ps()
                    for kk in range(2):
                        p.op("pe", lambda e: e.matmul(ps[:], wg[:, hd, kk, jj * 128:(jj + 1) * 128], xcb[:, kk, :],
                                                      start=(kk == 0), stop=(kk == 1)), reads=[wg, xcb], writes=[ps])
                    p.op("act", lambda e: e.activation(dstg[:, jj, :], ps[:], AF.Sigmoid, bias=bg[:, j:j + 1]),
                         reads=[ps, bg], writes=[dstg])
            for jj in range(2):
                j = 2 * hd + jj
                p.op("act", lambda e: e.activation(aa[:, jj, :], rr[:, jj, :], AF.Exp, scale=scol[:, j:j + 1]),
                     reads=[rr, scol], writes=[aa])
                p.op("act", lambda e: e.activation(a2[:, jj, :], rr[:, jj, :], AF.Exp, scale=scol2[:, j:j + 1]),
                     reads=[rr, scol2], writes=[a2])
            p.op("dve", lambda e: e.tensor_scalar(a2[:], a2[:], -1.0, 1.0, ALU.mult, ALU.add), reads=[a2], writes=[a2])
            p.op("act", lambda e: e.activation(a2[:], a2[:], AF.Sqrt), reads=[a2], writes=[a2])
            p.op("dve", lambda e: e.tensor_tensor(ii[:], ii[:], xc[:], ALU.mult), reads=[ii, xc], writes=[ii])
            p.op("dve", lambda e: e.tensor_tensor(ii[:], ii[:], a2[:], ALU.mult), reads=[ii, a2], writes=[ii])
            for jj in range(2):
                j = 2 * hd + jj
                p.op("dve", lambda e: e.tensor_tensor_scan(hs[:, jj, :], aa[:, jj, :], ii[:, jj, :], carry[:, j:j + 1],
                                                           ALU.mult, ALU.add), reads=[aa, ii, carry], writes=[hs])
                p.op("dve", lambda e: e.tensor_copy(carry[:, j:j + 1], hs[:, jj, BLK - 1:BLK]), reads=[hs], writes=[carry])
                p.op("dve", lambda e: e.tensor_tensor_scan(pscan[:], aa[:, jj, :], zeros[:], cprod[:, j:j + 1],
                                                           ALU.mult, ALU.add), reads=[aa, zeros, cprod], writes=[pscan])
                p.op("dve", lambda e: e.tensor_copy(cprod[:, j:j + 1], pscan[:, BLK - 1:BLK]), reads=[pscan], writes=[cprod])
            for jj in range(2):
                j = 2 * hd + jj
                ps = next_ps()
                for k in range(KC):
                    p.op("pe", lambda e: e.matmul(ps[:], win[:, k, j * 128:(j + 1) * 128], hT[:, k, :],
                                                  start=(k == 0), stop=(k == KC - 1)), reads=[win, hT], writes=[ps])
                p.op("act", lambda e: e.activation(gt[:, jj, :], ps[:], AF.Gelu), reads=[ps], writes=[gt])
            p.op("pool", lambda e: e.tensor_tensor(yT[:, 2 * hd:2 * hd + 2, :], gt[:], hs[:], ALU.mult),
                 reads=[gt, hs], writes=[yT])
        for m in range(KC):
            ps = next_ps()
            for k in range(KC):
                p.op("pe", lambda e: e.matmul(ps[:], wout[:, k, m * 128:(m + 1) * 128], yT[:, k, :],
                                              start=(k == 0), stop=(k == KC - 1)), reads=[wout, yT], writes=[ps])
            p.op("dve", lambda e: e.scalar_tensor_tensor(xb[:, m, :], ps[:], modt[:, 16 + m:17 + m], xb[:, m, :],
                                                         ALU.mult, ALU.add), reads=[ps, modt, xb], writes=[xb])
        p.dma("sp", xo[:, :, b * BLK:(b + 1) * BLK], xb[:], reads=[xb], is_output=True)
    p.dma("sp", sAo, cprod[:], reads=[cprod], is_output=True)
    p.dma("sp", sBo, carry[:], reads=[carry], is_output=True)
    p.finish()
    return nc


def colT(v, nk=None):
    v = np.asarray(v)
    nk = v.shape[0] // 128
    return np.ascontiguousarray(v.reshape(nk, 128).T)


def to_fm(xtok):
    Tn, Dn = xtok.shape
    return np.ascontiguousarray(xtok.T.reshape(Dn // 128, 128, Tn).transpose(1, 0, 2))


def from_fm(xfm):
    P, nk, Tn = xfm.shape
    return np.ascontiguousarray(xfm.transpose(1, 0, 2).reshape(nk * 128, Tn).T)


def lru_inputs(inp, l, r, xfm_all, mod_r, sums, use_mask):
    b, c = r // 4, r % 4
    xh = np.zeros((128, KC, 4), np.float32)
    if c > 0:
        xh[:, :, 1:4] = xfm_all[r - 1][:, :, T - 3:T]
    sumA = np.zeros((128, 4, KC), np.float32)
    sumB = np.zeros((128, 4, KC), np.float32)
    cmask = np.zeros((128, 4), np.float32)
    if use_mask:
        for j in range(4):
            sumA[:, j, :] = sums[b * 4 + j][0]
            sumB[:, j, :] = sums[b * 4 + j][1]
            if j < c:
                cmask[:, j] = 1.0
    return {
        "xT": xfm_all[r], "xh": xh,
        "flag": np.full((128, 1), 1.0 if c > 0 else 0.0, np.float32),
        "mod": np.ascontiguousarray(mod_r[:, 0:24]),
        "ng": colT(inp["norm_mix_g"][l]),
        "w_in": np.ascontiguousarray(inp["lru_w_in"][l]),
        "cw": np.ascontiguousarray(inp["lru_conv_w"][l].reshape(4, KC, 128).transpose(2, 1, 0)),
        "cb": colT(inp["lru_conv_b"][l]),
        "wa": np.ascontiguousarray(inp["lru_wa"][l]), "ba": colT(inp["lru_ba"][l]),
        "wx": np.ascontiguousarray(inp["lru_wx"][l]), "bx": colT(inp["lru_bx"][l]),
        "lam": colT(inp["lru_lambda"][l]),
        "w_out": np.ascontiguousarray(inp["lru_w_out"][l]),
        "sumA": sumA, "sumB": sumB, "cmask": cmask,
    }


NEXP = 16384
PAIR = 256
NPAIR = T // PAIR
THR_DELTA = 2e-5


def build_peer(npair=NPAIR):
    nc = bass.Bass("TRN2", target_bir_lowering=False)
    xT = ext_in(nc, "xT", [128, KC, T])
    modd = ext_in(nc, "mod", [128, 24])
    ngd = ext_in(nc, "ng", [128, KC])
    w_q = ext_in(nc, "w_q", [D, 2 * D])
    skd = ext_in(nc, "skT", [128, 16, 128])
    uT = ext_in(nc, "uT", [128, KC, NEXP], BF16)
    vb = ext_in(nc, "vb", [NEXP, D], BF16)
    xo = ext_out(nc, "xo", [128, KC, T])

    p = Prog(nc)
    cm = Common(p, need_ident=True, need_stage=False)
    wk = norm_work(p, KC, PAIR)
    modt = small_in(p, "modt", modd, [128, 24])
    ng = small_in(p, "ngt", ngd, [128, KC])
    A2 = mod_cols(p, "m2", modt, ng, 0, 1)
    skT = small_in(p, "skTf", skd, [128, 16, 128])
    wq_v = w_q.rearrange("(k q) m -> q k m", q=128)
    wqj = [p.sb("wqj%d" % i, [128, KC, 128], F32) for i in range(2)]
    hTf = p.sb("hTf", [128, KC, PAIR], F32)

    xp = p.sb("xp", [128, KC, PAIR], F32)
    hTp = p.sb("hTp", [128, KC, PAIR], BF16)
    qT = p.sb("qT", [128, 16, PAIR], F32)
    S_sb = [p.sb("S_sb%d" % i, [128, 16, 128], F32) for i in range(2)]
    th = [p.sb("th%d" % i, [128, 8, 128], F32) for i in range(2)]
    P1 = [p.sb("P1_%d" % i, [128, 8, 128], BF16) for i in range(2)]
    P2 = [p.sb("P2_%d" % i, [128, 8, 128], BF16) for i in range(2)]
    v16 = p.sb("v16", [128, 16, 16], F32)
    work = p.sb("work", [128, 128], F32)
    cand = p.sb("cand", [128, 8, 16, 16], F32)
    work2 = p.sb("work2", [128, 16, 16], F32)
    cv = p.sb("cv", [128, 8, 16], F32)
    thr = p.sb("thr", [128, 8], F32)
    dd = p.sb("dd", [128, 8, 16], F32)
    Z = p.sb("Z", [128, 8], F32)
    rZ = p.sb("rZ", [128, 8], F32)
    Psub = p.sb("Psub", [128, 16, 128], F32)
    G = [p.sb("G%d" % i, [128, 16, 128], BF16) for i in range(2)]
    mks = [p.sb("mk%d" % i, [128, 16, 128], BF16) for i in range(2)]
    Ets = [p.sb("Et%d" % i, [128, 16, 128], BF16) for i in range(2)]
    MEs = [p.sb("ME%d" % i, [128, 16, 128], BF16) for i in range(2)]
    uTg = [p.sb("uTg%d" % i, [128, KC, 512], BF16) for i in range(2)]
    vbg = [p.sb("vbg%d" % i, [128, 4, D], BF16) for i in range(2)]
    gl = [p.sb("gl%d" % i, [128, 512], BF16) for i in range(2)]
    Wt = [p.sb("Wt%d" % i, [128, 512], BF16) for i in range(2)]
    WT_sb = [p.sb("WT_sb%d" % i, [128, 2, 4, 128], BF16) for i in range(2)]
    ao = p.sb("ao", [128, D], F32)
    tmpo = p.sb("tmpo", [128, 4, 128], F32)

    acc = [p.ps("acc%d" % i, [128, D], F32) for i in range(2)]
    Sps = [p.ps("Sps%d" % i, [128, 512], F32) for i in range(2)]
    WTps = [p.ps("WTps%d" % i, [128, 2, 4, 128], BF16) for i in range(2)]

    ngrp = NEXP // 512
    SLAB = 16
    for pr in range(npair):
        c0 = pr * PAIR
        p.dma("sp", xp[:], xT[:, :, c0:c0 + PAIR], writes=[xp])
        norm_block(p, cm, xp, KC, PAIR, A2, modt, lambda k: hTf[:, k, :], hTf, wk, float(D), Sps[0])
        p.op("pool", lambda e: e.tensor_copy(hTp[:], hTf[:]), reads=[hTf], writes=[hTp])
        for j in range(16):
            ps = Sps[j % 2]
            wj = wqj[j % 2]
            p.dma("sp", wj[:], wq_v[:, :, j * 128:(j + 1) * 128], writes=[wj])
            for k in range(KC):
                p.op("pe", lambda e: e.matmul(ps[:, 0:PAIR], wj[:, k, :], hTf[:, k, :],
                                              start=(k == 0), stop=(k == KC - 1)), reads=[wj, hTf], writes=[ps])
            if j % 2 == 0:
                p.op("act", lambda e: e.activation(qT[:, j, :], ps[:, 0:PAIR], AF.Copy), reads=[ps], writes=[qT])
            else:
                p.op("dve", lambda e: e.tensor_copy(qT[:, j, :], ps[:, 0:PAIR]), reads=[ps], writes=[qT])
        for tt in range(2):
            S = S_sb[tt]
            for grp in range(4):
                ps = Sps[grp % 2]
                for jj in range(4):
                    j = grp * 4 + jj
                    p.op("pe", lambda e: e.matmul(ps[:, jj * 128:(jj + 1) * 128], qT[:, j, tt * 128:(tt + 1) * 128], skT[:, j, :],
                                                  start=True, stop=True), reads=[qT, skT], writes=[ps])
                p.op("act", lambda e: e.activation(S[:, grp * 4:grp * 4 + 4, :].rearrange("p j n -> p (j n)"), ps[:], AF.Copy),
                     reads=[ps], writes=[S])
            for j in range(16):
                p.op("dve", lambda e: e.max(v16[:, j, 0:8], S[:, j, :]), reads=[S], writes=[v16])
                p.op("dve", lambda e: e.match_replace(work[:], v16[:, j, 0:8], S[:, j, :], -1e30), reads=[S, v16], writes=[work])
                p.op("dve", lambda e: e.max(v16[:, j, 8:16], work[:]), reads=[work], writes=[v16])
            v4 = v16[:].rearrange("p (h two) k -> p h two k", two=2)
            p.op("dve", lambda e: e.tensor_tensor(cand[:], v4[:, :, 0, :].unsqueeze(3).to_broadcast([128, 8, 16, 16]),
                                                  v4[:, :, 1, :].unsqueeze(2).to_broadcast([128, 8, 16, 16]), ALU.add),
                 reads=[v16], writes=[cand])
            for h in range(8):
                p.op("dve", lambda e: e.max(cv[:, h, 0:8], cand[:, h]), reads=[cand], writes=[cv])
                p.op("dve", lambda e: e.match_replace(work2[:], cv[:, h, 0:8], cand[:, h], -1e30), reads=[cand, cv], writes=[work2])
                p.op("dve", lambda e: e.max(cv[:, h, 8:16], work2[:]), reads=[work2], writes=[cv])
            p.op("dve", lambda e: e.tensor_scalar(thr[:], cv[:, :, 15], -THR_DELTA, None, ALU.add), reads=[cv], writes=[thr])
            p.op("dve", lambda e: e.tensor_tensor(dd[:], cv[:], cv[:, :, 0:1].to_broadcast([128, 8, 16]), ALU.subtract),
                 reads=[cv], writes=[dd])
            p.op("act", lambda e: e.activation(dd[:], dd[:], AF.Exp), reads=[dd], writes=[dd])
            p.op("dve", lambda e: e.tensor_reduce(Z[:], dd[:], AX.X, ALU.add), reads=[dd], writes=[Z])
            p.op("dve", lambda e: e.reciprocal(rZ[:], Z[:]), reads=[Z], writes=[rZ])
            p.op("dve", lambda e: e.tensor_tensor(Psub[:], S[:], v16[:, :, 0:1].to_broadcast([128, 16, 128]), ALU.subtract),
                 reads=[S, v16], writes=[Psub])
            p.op("act", lambda e: e.activation(Psub[:], Psub[:], AF.Exp), reads=[Psub], writes=[Psub])
            Pv = Psub[:].rearrange("p (h two) n -> p h two n", two=2)
            Sv = S[:].rearrange("p (h two) n -> p h two n", two=2)
            p.op("dve", lambda e: e.tensor_tensor(P1[tt][:], Pv[:, :, 0, :], rZ[:].unsqueeze(2).to_broadcast([128, 8, 128]), ALU.mult),
                 reads=[Psub, rZ], writes=[P1[tt]])
            p.op("pool", lambda e: e.tensor_copy(P2[tt][:], Pv[:, :, 1, :]), reads=[Psub], writes=[P2[tt]])
            p.op("dve", lambda e: e.tensor_tensor(th[tt][:], thr[:].unsqueeze(2).to_broadcast([128, 8, 128]), Sv[:, :, 0, :], ALU.subtract),
                 reads=[thr, S], writes=[th[tt]])

        def load_grp(g):
            e0 = g * 512
            u = uTg[g % 2]
            v = vbg[g % 2]
            p.dma("sp", u[:], uT[:, :, e0:e0 + 512], writes=[u])
            p.dma("sp", v[:], vb[e0:e0 + 512, :].rearrange("(c q) d -> q c d", q=128), writes=[v])

        load_grp(0)
        for g in range(ngrp):
            if g % 4 == 0:
                slab = g // 4
                e1a = slab * SLAB
                for tt in range(2):
                    Sv = S_sb[tt][:].rearrange("p (h two) n -> p h two n", two=2)
                    for h in range(8):
                        mk, Et, ME = mks[h % 2], Ets[h % 2], MEs[h % 2]
                        p.op("dve", lambda e: e.tensor_tensor(mk[:], Sv[:, h, 1, :].unsqueeze(1).to_broadcast([128, SLAB, 128]),
                                                              th[tt][:, h, e1a:e1a + SLAB].unsqueeze(2).to_broadcast([128, SLAB, 128]),
                                                              ALU.is_ge), reads=[S_sb[tt], th[tt]], writes=[mk])
                        p.op("pool", lambda e: e.tensor_tensor(Et[:], P2[tt][:, h, :].unsqueeze(1).to_broadcast([128, SLAB, 128]),
                                                               P1[tt][:, h, e1a:e1a + SLAB].unsqueeze(2).to_broadcast([128, SLAB, 128]),
                                                               ALU.mult), reads=[P1[tt], P2[tt]], writes=[Et])
                        if h == 0:
                            p.op("dve", lambda e: e.tensor_tensor(G[tt][:], mk[:], Et[:], ALU.mult), reads=[mk, Et], writes=[G[tt]])
                        else:
                            p.op("pool", lambda e: e.tensor_tensor(ME[:], mk[:], Et[:], ALU.mult), reads=[mk, Et], writes=[ME])
                            p.op("dve", lambda e: e.tensor_tensor(G[tt][:], G[tt][:], ME[:], ALU.add), reads=[G[tt], ME], writes=[G[tt]])
            if g + 1 < ngrp:
                load_grp(g + 1)
            u = uTg[g % 2]
            v = vbg[g % 2]
            gi = g % 4
            wtp = WTps[g % 2]
            wts = WT_sb[g % 2]
            for tt in range(2):
                ps = Sps[tt]
                for k in range(KC):
                    p.op("pe", lambda e: e.matmul(ps[:], hTp[:, k, tt * 128:(tt + 1) * 128], u[:, k, :],
                                                  start=(k == 0), stop=(k == KC - 1)), reads=[hTp, u], writes=[ps])
                p.op("act", lambda e: e.activation(gl[tt][:], ps[:], AF.Gelu), reads=[ps], writes=[gl[tt]])
                p.op("dve", lambda e: e.tensor_tensor(Wt[tt][:], gl[tt][:],
                                                      G[tt][:, gi * 4:gi * 4 + 4, :].rearrange("p a n -> p (a n)"), ALU.mult),
                     reads=[gl[tt], G[tt]], writes=[Wt[tt]])
                for c in range(4):
                    p.op("pe", lambda e: e.transpose(wtp[:, tt, c, :], Wt[tt][:, c * 128:(c + 1) * 128], cm.ident[:]),
                         reads=[Wt[tt], cm.ident], writes=[wtp])
            p.op("act", lambda e: e.activation(wts[:].rearrange("p a c n -> p (a c n)"), wtp[:].rearrange("p a c n -> p (a c n)"), AF.Copy),
                 reads=[wtp], writes=[wts])
            for tt in range(2):
                for c in range(4):
                    for half in range(2):
                        p.op("pe", lambda e: e.matmul(acc[tt][:, half * 512:(half + 1) * 512], wts[:, tt, c, :], v[:, c, half * 512:(half + 1) * 512],
                                                      start=(g == 0 and c == 0), stop=(g == ngrp - 1 and c == 3)),
                             reads=[wts, v], writes=[acc[tt]])
        for tt in range(2):
            p.op("act", lambda e: e.activation(ao[:], acc[tt][:], AF.Copy), reads=[acc[tt]], writes=[ao])
            for half in range(2):
                ps = Sps[half]
                for kk in range(4):
                    k = half * 4 + kk
                    p.op("pe", lambda e: e.transpose(ps[:, kk * 128:(kk + 1) * 128], ao[:, k * 128:(k + 1) * 128], cm.identf[:]),
                         reads=[ao, cm.identf], writes=[ps])
                p.op("dve", lambda e: e.tensor_tensor(tmpo[:], ps[:].rearrange("p (a n) -> p a n", a=4),
                                                      modt[:, 16 + half * 4:20 + half * 4].unsqueeze(2).to_broadcast([128, 4, 128]), ALU.mult),
                     reads=[ps, modt], writes=[tmpo])
                p.op("pool", lambda e: e.tensor_tensor(xp[:, half * 4:half * 4 + 4, tt * 128:(tt + 1) * 128], tmpo[:],
                                                       xp[:, half * 4:half * 4 + 4, tt * 128:(tt + 1) * 128], ALU.add),
                     reads=[tmpo, xp], writes=[xp])
        p.dma("sp", xo[:, :, c0:c0 + PAIR], xp[:], reads=[xp], is_output=True)
    p.finish()
    print("peer nins", p.nins)
    return nc


def peer_inputs(inp, l, xfm, mod_r, uT_l, vb_l):
    sk = inp["peer_sub_keys"][l]
    skT = np.ascontiguousarray(sk.reshape(16, 128, 128).transpose(2, 0, 1))
    return {
        "xT": xfm, "mod": np.ascontiguousarray(mod_r[:, 24:48]),
        "ng": colT(inp["norm_ffn_g"][l]),
        "w_q": np.ascontiguousarray(inp["peer_w_q"][l]),
        "skT": skT, "uT": uT_l, "vb": vb_l,
    }


TWO_PI = 6.283185307179586


def rope_tables(p, posd, freqd, n, scale, name, pb=0):
    P_ = pb + 32
    sl = slice(pb, pb + 32)
    posi = p.sb(name + "_posi", [P_, n], I32)
    p.dma("sp", posi[sl, :], posd.partition_broadcast(32), writes=[posi])
    fr = p.sb(name + "_fr", [P_, 1], F32)
    p.dma("sp", fr[sl, :], freqd, writes=[fr])
    negpi = p.sb(name + "_negpi", [P_, 1], F32)
    p.op("pool", lambda e: e.memset(negpi[sl, :], -3.141592653589793), writes=[negpi])
    ang = p.sb(name + "_ang", [P_, n], F32)
    p.op("dve", lambda e: e.tensor_copy(ang[sl, :], posi[sl, :]), reads=[posi], writes=[ang])
    p.op("dve", lambda e: e.tensor_scalar(ang[sl, :], ang[sl, :], fr[sl, 0:1], None, ALU.mult), reads=[ang, fr], writes=[ang])
    outs = []
    ki = posi
    kf = p.sb(name + "_kf", [P_, n], F32)
    for (nm, off) in (("sin", 0.5), ("cos", 0.75)):
        u = p.sb(name + "_u" + nm, [P_, n], F32)
        p.op("dve", lambda e: e.tensor_scalar(u[sl, :], ang[sl, :], 1.0 / TWO_PI, off, ALU.mult, ALU.add), reads=[ang], writes=[u])
        p.op("dve", lambda e: e.tensor_copy(ki[sl, :], u[sl, :]), reads=[u], writes=[ki])
        p.op("dve", lambda e: e.tensor_copy(kf[sl, :], ki[sl, :]), reads=[ki], writes=[kf])
        p.op("dve", lambda e: e.tensor_tensor(u[sl, :], u[sl, :], kf[sl, :], ALU.subtract), reads=[u, kf], writes=[u])
        p.op("dve", lambda e: e.tensor_scalar(kf[sl, :], u[sl, :], 0.0, None, ALU.is_lt), reads=[u], writes=[kf])
        p.op("dve", lambda e: e.tensor_tensor(u[sl, :], u[sl, :], kf[sl, :], ALU.add), reads=[u, kf], writes=[u])
        p.op("dve", lambda e: e.tensor_scalar(u[sl, :], u[sl, :], 0.9999999, None, ALU.min), reads=[u], writes=[u])
        p.op("act", lambda e: e.activation(u[sl, :], u[sl, :], AF.Sin, scale=TWO_PI, bias=negpi[sl, 0:1]), reads=[u, negpi], writes=[u])
        if scale != 1.0:
            p.op("dve", lambda e: e.tensor_scalar(u[sl, :], u[sl, :], float(scale), None, ALU.mult), reads=[u], writes=[u])
        outs.append(u)
    return outs[1], outs[0]


def build_kv():
    nc = bass.Bass("TRN2", target_bir_lowering=False)
    xT = ext_in(nc, "xT", [128, KC, T])
    modd = ext_in(nc, "mod", [128, 16])
    ngd = ext_in(nc, "ng", [128, KC])
    w_dkv = ext_in(nc, "w_dkv", [D, 256])
    w_kr = ext_in(nc, "w_kr", [D, 32])
    lgd = ext_in(nc, "lg", [128, 2])
    w_uk = ext_in(nc, "w_uk", [256, D])
    w_uv = ext_in(nc, "w_uv", [256, D])
    posd = ext_in(nc, "pos", [1, T], I32)
    freqd = ext_in(nc, "freq", [32, 1])
    kTo = ext_out(nc, "kT", [128, KC, T], BF16)
    kro = ext_out(nc, "krT", [32, T], BF16)
    vo = ext_out(nc, "v", [128, T // 128, D], BF16)

    p = Prog(nc)
    cm = Common(p)
    wk = norm_work(p)
    modt = small_in(p, "modt", modd, [128, 16])
    ng = small_in(p, "ngt", ngd, [128, KC])
    lg = small_in(p, "lgt", lgd, [128, 2])
    zer = p.sb("zer", [128, KC], F32)
    p.op("pool", lambda e: e.memset(zer[:], 0.0), writes=[zer])
    A = mod_cols(p, "mkv", modt, ng, 0, 1)
    cos, sin = rope_tables(p, posd, freqd, T, 1.0, "rk")
    wdkv = p.sb("wdkv", [128, KC, 256], BF16)
    wkr = p.sb("wkr", [128, KC, 32], BF16)
    wsw = p.sb("wsw", [128, KC, 32], BF16)
    wuk = p.sb("wuk", [128, 2, D], BF16)
    wuv = p.sb("wuv", [128, 2, D], BF16)
    load_cast(p, cm, w_dkv, lambda k: wdkv[:, k, :], KC, 256, dst=wdkv)
    load_cast(p, cm, w_uk, lambda k: wuk[:, k, :], 2, D, dst=wuk)
    load_cast(p, cm, w_uv, lambda k: wuv[:, k, :], 2, D, dst=wuv)
    st = cm.next_stage()
    p.dma("sp", st[:, 0:256].rearrange("q (k m) -> q k m", k=KC), w_kr.rearrange("(k q) m -> q k m", q=128), writes=[st])
    stv = st[:, 0:256].rearrange("q (k m) -> q k m", k=KC)
    p.op("pool", lambda e: e.tensor_copy(wkr[:], stv), reads=[st], writes=[wkr])
    p.op("dve", lambda e: e.tensor_scalar(wsw[:, :, 0:16], stv[:, :, 16:32], -1.0, None, ALU.mult), reads=[st], writes=[wsw])
    p.op("dve", lambda e: e.tensor_copy(wsw[:, :, 16:32], stv[:, :, 0:16]), reads=[st], writes=[wsw])

    xblk = [p.sb("xblk%d" % i, [128, KC, BLK], F32) for i in range(2)]
    hT = p.sb("hT", [128, KC, BLK], BF16)
    latf = p.sb("latf", [128, 2, BLK], F32)
    ckv = p.sb("ckv", [128, 2, BLK], BF16)
    kTs = [p.sb("kTs%d" % i, [128, KC, BLK], BF16) for i in range(2)]
    krs = p.sb("krs", [32, BLK], BF16)
    t1 = p.sb("t1", [32, BLK], F32)
    t2 = p.sb("t2", [32, BLK], F32)
    vs = [p.sb("vs%d" % i, [128, D], BF16) for i in range(2)]
    psn = p.ps("psn", [128, 512], F32)
    pmm = [p.ps("pmm%d" % i, [128, 512], F32) for i in range(5)]
    pmi = [0]

    def next_ps():
        t = pmm[pmi[0] % 5]
        pmi[0] += 1
        return t

    for b in range(NBLK):
        c0 = b * BLK
        xb = xblk[b % 2]
        p.dma("sp", xb[:], xT[:, :, c0:c0 + BLK], writes=[xb])
        norm_block(p, cm, xb, KC, BLK, A, modt, lambda k: hT[:, k, :], hT, wk, float(D), psn)
        for m in range(2):
            ps = next_ps()
            for k in range(KC):
                p.op("pe", lambda e: e.matmul(ps[:], wdkv[:, k, m * 128:(m + 1) * 128], hT[:, k, :], start=(k == 0), stop=(k == KC - 1)),
                     reads=[wdkv, hT], writes=[ps])
            p.op("act", lambda e: e.activation(latf[:, m, :], ps[:], AF.Copy), reads=[ps], writes=[latf])
        norm_block(p, cm, latf, 2, BLK, lg, zer, lambda k: ckv[:, k, :], ckv, wk, 256.0, psn)
        ps1 = next_ps()
        ps2 = next_ps()
        for k in range(KC):
            p.op("pe", lambda e: e.matmul(ps1[0:32, :], wkr[:, k, :], hT[:, k, :], start=(k == 0), stop=(k == KC - 1)),
                 reads=[wkr, hT], writes=[ps1])
        for k in range(KC):
            p.op("pe", lambda e: e.matmul(ps2[0:32, :], wsw[:, k, :], hT[:, k, :], start=(k == 0), stop=(k == KC - 1)),
                 reads=[wsw, hT], writes=[ps2])
        p.op("dve", lambda e: e.tensor_tensor(t1[:], ps1[0:32, :], cos[:, c0:c0 + BLK], ALU.mult), reads=[ps1, cos], writes=[t1])
        p.op("dve", lambda e: e.tensor_tensor(t2[:], ps2[0:32, :], sin[:, c0:c0 + BLK], ALU.mult), reads=[ps2, sin], writes=[t2])
        p.op("dve", lambda e: e.tensor_tensor(krs[:], t1[:], t2[:], ALU.add), reads=[t1, t2], writes=[krs])
        p.dma("sp", kro[:, c0:c0 + BLK], krs[:], reads=[krs], is_output=True)
        kt = kTs[b % 2]
        for m in range(KC):
            ps = next_ps()
            for k in range(2):
                p.op("pe", lambda e: e.matmul(ps[:], wuk[:, k, m * 128:(m + 1) * 128], ckv[:, k, :], start=(k == 0), stop=(k == 1)),
                     reads=[wuk, ckv], writes=[ps])
            if m % 2 == 0:
                p.op("act", lambda e: e.activation(kt[:, m, :], ps[:], AF.Copy), reads=[ps], writes=[kt])
            else:
                p.op("dve", lambda e: e.tensor_copy(kt[:, m, :], ps[:]), reads=[ps], writes=[kt])
        p.dma("sp", kTo[:, :, c0:c0 + BLK], kt[:], reads=[kt], is_output=True)
        for tl in range(4):
            vt = vs[tl % 2]
            for half in range(2):
                ps = next_ps()
                for k in range(2):
                    p.op("pe", lambda e: e.matmul(ps[:], ckv[:, k, tl * 128:(tl + 1) * 128], wuv[:, k, half * 512:(half + 1) * 512],
                                                  start=(k == 0), stop=(k == 1)), reads=[ckv, wuv], writes=[ps])
                if half == 0:
                    p.op("act", lambda e: e.activation(vt[:, 0:512], ps[:], AF.Copy), reads=[ps], writes=[vt])
                else:
                    p.op("dve", lambda e: e.tensor_copy(vt[:, 512:1024], ps[:]), reads=[ps], writes=[vt])
            p.dma("sp", vo[:, b * 4 + tl, :], vt[:], reads=[vt], is_output=True)
    p.finish()
    return nc


ATTN_SCALE = 1.0 / (96.0 ** 0.5)


def build_q():
    nc = bass.Bass("TRN2", target_bir_lowering=False)
    xT = ext_in(nc, "xT", [128, KC, T])
    modd = ext_in(nc, "mod", [128, 24])
    ngd = ext_in(nc, "ng", [128, KC])
    w_dq = ext_in(nc, "w_dq", [D, 384])
    qgd = ext_in(nc, "qg", [128, 3])
    w_uq = ext_in(nc, "w_uq", [384, 1536])
    posd = ext_in(nc, "pos", [1, T], I32)
    freqd = ext_in(nc, "freq", [32, 1])
    qo = ext_out(nc, "q", [96, 16, T], BF16)

    p = Prog(nc)
    cm = Common(p)
    wk = norm_work(p)
    modt = small_in(p, "modt", modd, [128, 24])
    ng = small_in(p, "ngt", ngd, [128, KC])
    qg = small_in(p, "qgt", qgd, [128, 3])
    zer = p.sb("zer", [128, KC], F32)
    p.op("pool", lambda e: e.memset(zer[:], 0.0), writes=[zer])
    A = mod_cols(p, "mq", modt, ng, 0, 1)
    cos, sin = rope_tables(p, posd, freqd, T, ATTN_SCALE, "rq", pb=64)
    wdq = p.sb("wdq", [128, KC, 384], BF16)
    load_cast(p, cm, w_dq, lambda k: wdq[:, k, :], KC, 384, dst=wdq)
    wqh = p.sb("wqh", [128, 3, 16, 96], BF16)
    wsw = p.sb("wsw", [128, 3, 16, 96], BF16)
    p.op("pool", lambda e: e.memset(wsw[:], 0.0), writes=[wsw])
    for k in range(3):
        st = cm.next_stage()
        p.dma("sp", st[:, 0:1536], w_uq[k * 128:(k + 1) * 128, :], writes=[st])
        sv = st[:, 0:1536].rearrange("q (h c) -> q h c", h=16)
        p.op("pool", lambda e: e.tensor_copy(wqh[:, k, :, :], sv), reads=[st], writes=[wqh])
        p.op("dve", lambda e: e.tensor_scalar(wsw[:, k, :, 64:80], sv[:, :, 80:96], -1.0, None, ALU.mult), reads=[st], writes=[wsw])
        p.op("dve", lambda e: e.tensor_copy(wsw[:, k, :, 80:96], sv[:, :, 64:80]), reads=[st], writes=[wsw])

    xblk = [p.sb("xblk%d" % i, [128, KC, BLK], F32) for i in range(2)]
    hT = p.sb("hT", [128, KC, BLK], BF16)
    qlf = p.sb("qlf", [128, 3, BLK], F32)
    ql = p.sb("ql", [128, 3, BLK], BF16)
    qs = [p.sb("qs%d" % i, [96, 16, BLK], BF16) for i in range(2)]
    t1 = p.sb("t1", [96, BLK], F32)
    t2 = p.sb("t2", [96, BLK], F32)
    psn = p.ps("psn", [128, 512], F32)
    pmm = [p.ps("pmm%d" % i, [128, 512], F32) for i in range(6)]
    pmi = [0]

    def next_ps():
        t = pmm[pmi[0] % 6]
        pmi[0] += 1
        return t

    for b in range(NBLK):
        c0 = b * BLK
        xb = xblk[b % 2]
        p.dma("sp", xb[:], xT[:, :, c0:c0 + BLK], writes=[xb])
        norm_block(p, cm, xb, KC, BLK, A, modt, lambda k: hT[:, k, :], hT, wk, float(D), psn)
        for m in range(3):
            ps = next_ps()
            for k in range(KC):
                p.op("pe", lambda e: e.matmul(ps[:], wdq[:, k, m * 128:(m + 1) * 128], hT[:, k, :], start=(k == 0), stop=(k == KC - 1)),
                     reads=[wdq, hT], writes=[ps])
            p.op("act", lambda e: e.activation(qlf[:, m, :], ps[:], AF.Copy), reads=[ps], writes=[qlf])
        norm_block(p, cm, qlf, 3, BLK, qg, zer, lambda k: ql[:, k, :], ql, wk, 384.0, psn)
        q_s = qs[b % 2]
        for h in range(16):
            ps1 = next_ps()
            ps2 = next_ps()
            for k in range(3):
                p.op("pe", lambda e: e.matmul(ps1[0:96, :], wqh[:, k, h, :], ql[:, k, :], start=(k == 0), stop=(k == 2)),
                     reads=[wqh, ql], writes=[ps1])
            for k in range(3):
                p.op("pe", lambda e: e.matmul(ps2[0:96, :], wsw[:, k, h, :], ql[:, k, :], start=(k == 0), stop=(k == 2)),
                     reads=[wsw, ql], writes=[ps2])
            p.op("act", lambda e: e.activation(q_s[0:64, h, :], ps1[0:64, :], AF.Copy, scale=ATTN_SCALE), reads=[ps1], writes=[q_s])
            p.op("dve", lambda e: e.tensor_tensor(t1[64:96, :], ps1[64:96, :], cos[64:96, c0:c0 + BLK], ALU.mult), reads=[ps1, cos], writes=[t1])
            p.op("dve", lambda e: e.tensor_tensor(t2[64:96, :], ps2[64:96, :], sin[64:96, c0:c0 + BLK], ALU.mult), reads=[ps2, sin], writes=[t2])
            p.op("pool", lambda e: e.tensor_tensor(q_s[64:96, h, :], t1[64:96, :], t2[64:96, :], ALU.add), reads=[t1, t2], writes=[q_s])
        p.dma("sp", qo[:, :, c0:c0 + BLK], q_s[:], reads=[q_s], is_output=True)
    p.finish()
    return nc


HPC = 4
QB = 512
NQB = SEQ // QB
NKT = SEQ // 128


def build_att(nqb=NQB):
    nc = bass.Bass("TRN2", target_bir_lowering=False)
    qd = ext_in(nc, "q", [96, HPC, SEQ], BF16)
    kd = ext_in(nc, "k", [96, HPC, SEQ], BF16)
    vd = ext_in(nc, "v", [128, HPC, NKT, 64], BF16)
    od = ext_out(nc, "o", [64, HPC, SEQ], BF16)

    p = Prog(nc)
    onesf = p.sb("onesf", [128, 64], F32)
    p.op("pool", lambda e: e.memset(onesf[:], 1.0), writes=[onesf])
    masks = p.sb("masks", [128, 4, QB], BF16)
    mi = p.sb("mi", [128, QB], F32)
    for d in range(4):
        p.op("pool", lambda e: e.iota(mi[:], [[1, QB]], base=-128 * d, channel_multiplier=-1,
                                      allow_small_or_imprecise_dtypes=True), writes=[mi])
        p.op("dve", lambda e: e.tensor_scalar(masks[:, d, :], mi[:], 0.0, None, ALU.is_ge), reads=[mi], writes=[masks])
    qh = [p.sb("qh%d" % i, [96, SEQ], BF16) for i in range(2)]
    kh = [p.sb("kh%d" % i, [96, SEQ], BF16) for i in range(2)]
    vh = [p.sb("vh%d" % i, [128, NKT, 65], BF16) for i in range(2)]
    for i in range(2):
        p.op("pool", lambda e: e.memset(vh[i][:, :, 64:65], 1.0), writes=[vh[i]])
    PT = [p.sb("PT%d" % i, [128, QB], BF16) for i in range(4)]
    rden = p.sb("rden", [128, QB], F32)
    num = p.sb("num", [64, QB], F32)
    ob = [p.sb("ob%d" % i, [64, QB], BF16) for i in range(2)]
    sps = [p.ps("sps%d" % i, [128, QB], F32) for i in range(4)]
    accs = [p.ps("accs%d" % i, [128, QB], F32) for i in range(2)]
    pbc = p.ps("pbc", [64, QB], F32)

    it = 0
    for h in range(HPC):
        q_ = qh[h % 2]
        k_ = kh[h % 2]
        v_ = vh[h % 2]
        p.dma("sp", q_[:], qd[:, h, :], writes=[q_])
        p.dma("sp", k_[:], kd[:, h, :], writes=[k_])
        p.dma("sp", v_[:, :, 0:64], vd[:, h, :, :], writes=[v_])
        for qb in range(nqb):
            acc = accs[qb % 2]
            nkt = 4 * (qb + 1)
            for kt in range(nkt):
                ps = sps[it % 4]
                pt = PT[it % 4]
                it += 1
                p.op("pe", lambda e: e.matmul(ps[:], k_[:, kt * 128:(kt + 1) * 128], q_[:, qb * QB:(qb + 1) * QB], start=True, stop=True),
                     reads=[k_, q_], writes=[ps])
                p.op("act", lambda e: e.activation(pt[:], ps[:], AF.Exp), reads=[ps], writes=[pt])
                d = kt - 4 * qb
                if d >= 0:
                    p.op("dve", lambda e: e.tensor_tensor(pt[:], pt[:], masks[:, d, :], ALU.mult), reads=[pt, masks], writes=[pt])
                p.op("pe", lambda e: e.matmul(acc[0:65, :], v_[:, kt, :], pt[:], start=(kt == 0), stop=(kt == nkt - 1)),
                     reads=[v_, pt], writes=[acc])
            p.op("dve", lambda e: e.reciprocal(rden[64:65, :], acc[64:65, :]), reads=[acc], writes=[rden])
            p.op("pe", lambda e: e.matmul(pbc[:], onesf[64:65, 0:64], rden[64:65, :], start=True, stop=True),
                 reads=[onesf, rden], writes=[pbc])
            p.op("act", lambda e: e.activation(num[:], acc[0:64, :], AF.Copy), reads=[acc], writes=[num])
            o_ = ob[qb % 2]
            p.op("dve", lambda e: e.tensor_tensor(o_[:], num[:], pbc[:], ALU.mult), reads=[num, pbc], writes=[o_])
            p.dma("sp", od[:, h, qb * QB:(qb + 1) * QB], o_[:], reads=[o_], is_output=True)
    p.finish()
    print("att nins", p.nins)
    return nc


def build_o():
    nc = bass.Bass("TRN2", target_bir_lowering=False)
    xT = ext_in(nc, "xT", [128, KC, T])
    aT = ext_in(nc, "aT", [128, KC, T], BF16)
    modd = ext_in(nc, "mod", [128, 24])
    w_o = ext_in(nc, "w_o", [D, D])
    xo = ext_out(nc, "xo", [128, KC, T])
    p = Prog(nc)
    cm = Common(p)
    modt = small_in(p, "modt", modd, [128, 24])
    wo = p.sb("wo", [128, KC, D], BF16)
    load_cast(p, cm, w_o, lambda k: wo[:, k, :], KC, D, dst=wo)
    xblk = [p.sb("xblk%d" % i, [128, KC, BLK], F32) for i in range(2)]
    ablk = [p.sb("ablk%d" % i, [128, KC, BLK], BF16) for i in range(2)]
    pmm = [p.ps("pmm%d" % i, [128, 512], F32) for i in range(4)]
    for b in range(NBLK):
        c0 = b * BLK
        xb = xblk[b % 2]
        ab = ablk[b % 2]
        p.dma("sp", xb[:], xT[:, :, c0:c0 + BLK], writes=[xb])
        p.dma("sp", ab[:], aT[:, :, c0:c0 + BLK], writes=[ab])
        for m in range(KC):
            ps = pmm[m % 4]
            for k in range(KC):
                p.op("pe", lambda e: e.matmul(ps[:], wo[:, k, m * 128:(m + 1) * 128], ab[:, k, :], start=(k == 0), stop=(k == KC - 1)),
                     reads=[wo, ab], writes=[ps])
            p.op("dve", lambda e: e.scalar_tensor_tensor(xb[:, m, :], ps[:], modt[:, 16 + m:17 + m], xb[:, m, :], ALU.mult, ALU.add),
                 reads=[ps, modt, xb], writes=[xb])
        p.dma("sp", xo[:, :, c0:c0 + BLK], xb[:], reads=[xb], is_output=True)
    p.finish()
    return nc


def build_fin():
    nc = bass.Bass("TRN2", target_bir_lowering=False)
    xT = ext_in(nc, "xT", [128, KC, T])
    gd = ext_in(nc, "g", [128, KC])
    xo = ext_out(nc, "xo", [128, KC, T])
    p = Prog(nc)
    cm = Common(p, need_stage=False)
    wk = norm_work(p)
    g = small_in(p, "gt", gd, [128, KC])
    zer = p.sb("zer", [128, KC], F32)
    p.op("pool", lambda e: e.memset(zer[:], 0.0), writes=[zer])
    xblk = [p.sb("xblk%d" % i, [128, KC, BLK], F32) for i in range(2)]
    oblk = [p.sb("oblk%d" % i, [128, KC, BLK], F32) for i in range(2)]
    psn = p.ps("psn", [128, 512], F32)
    for b in range(NBLK):
        c0 = b * BLK
        xb = xblk[b % 2]
        o_ = oblk[b % 2]
        p.dma("sp", xb[:], xT[:, :, c0:c0 + BLK], writes=[xb])
        norm_block(p, cm, xb, KC, BLK, g, zer, lambda k: o_[:, k, :], o_, wk, float(D), psn)
        p.dma("sp", xo[:, :, c0:c0 + BLK], o_[:], reads=[o_], is_output=True)
    p.finish()
    return nc


NCORES = 8


import os
import time as _time


def _run(nc, maps):
    t0 = _time.time()
    res = run_bass_kernel_spmd(nc, maps, core_ids=list(range(NCORES)))
    if os.environ.get("KDBG_DIR"):
        print("launch done in %.1fs" % (_time.time() - t0), flush=True)
    return res.results


def _dbg(name, arrs):
    d = os.environ.get("KDBG_DIR")
    if d:
        np.save(os.path.join(d, name + ".npy"), np.stack([np.asarray(a).astype(np.float32) for a in arrs]))


def kernel(**inputs):
    inp = {k: np.asarray(v) for k, v in inputs.items()}
    x = inp["x"].astype(np.float32, copy=False)
    c = inp["c"].astype(np.float32, copy=False)
    pos = inp["positions"].astype(np.int32, copy=False)
    depth = inp["ada_w"].shape[0]
    n_a = inp["lru_w_in"].shape[0]

    maps = []
    for r in range(NCORES):
        b, l = r // 4, r % 4
        maps.append({"W": np.ascontiguousarray(inp["ada_w"][l]), "cT": colT(c[b]),
                     "bT": np.ascontiguousarray(inp["ada_b"][l].reshape(48, 128).T)})
    res = _run(build_mod(), maps)
    mods = [[res[b * 4 + l]["out"] for l in range(depth)] for b in range(2)]
    wkv = np.zeros((D, 6144), np.float32)
    wkv[:, :2048] = inp["kv_ada_w"]
    bkv = np.zeros((6144,), np.float32)
    bkv[:2048] = inp["kv_ada_b"]
    maps = [{"W": wkv, "cT": colT(c[r // 4]), "bT": np.ascontiguousarray(bkv.reshape(48, 128).T)} for r in range(NCORES)]
    res = _run(build_mod(), maps)
    modkv = [np.ascontiguousarray(res[b * 4]["out"][:, 0:16]) for b in range(2)]

    maps = []
    for r in range(NCORES):
        maps.append({"U": np.ascontiguousarray(inp["peer_u"][:, r * 2048:(r + 1) * 2048, :].reshape(depth * 2048, D)),
                     "V": np.ascontiguousarray(inp["peer_v"][:, r * 2048:(r + 1) * 2048, :].reshape(depth * 2048, D))})
    res = _run(build_conv(), maps)
    uT_l = [np.concatenate([res[r]["uT"][:, :, l * 2048:(l + 1) * 2048] for r in range(NCORES)], axis=2) for l in range(depth)]
    vb_l = [np.concatenate([res[r]["vb"][l * 2048:(l + 1) * 2048] for r in range(NCORES)], axis=0) for l in range(depth)]
    del res

    freq = (10000.0 ** (-np.arange(16, dtype=np.float32) / 16.0)).astype(np.float32)
    freq32 = np.ascontiguousarray(np.concatenate([freq, freq]).reshape(32, 1))

    xfm = [to_fm(x[r // 4, (r % 4) * T:(r % 4 + 1) * T]) for r in range(NCORES)]
    posc = [np.ascontiguousarray(pos[r // 4, (r % 4) * T:(r % 4 + 1) * T].reshape(1, T)) for r in range(NCORES)]

    kv = None
    for l in range(depth):
        if l == n_a:
            maps = []
            for r in range(NCORES):
                b = r // 4
                maps.append({"xT": xfm[r], "mod": modkv[b], "ng": colT(inp["kv_norm_g"]),
                             "w_dkv": np.ascontiguousarray(inp["mla_w_dkv"]), "w_kr": np.ascontiguousarray(inp["mla_w_kr"]),
                             "lg": colT(inp["mla_kv_latent_g"]), "w_uk": np.ascontiguousarray(inp["mla_w_uk"]),
                             "w_uv": np.ascontiguousarray(inp["mla_w_uv"]), "pos": posc[r], "freq": freq32})
            res = _run(build_kv(), maps)
            kv = []
            for b in range(2):
                kn = np.concatenate([res[b * 4 + cc]["kT"] for cc in range(4)], axis=2)
                kn = kn.transpose(1, 0, 2).reshape(D, SEQ)
                kr = np.concatenate([res[b * 4 + cc]["krT"] for cc in range(4)], axis=1)
                vv = np.concatenate([res[b * 4 + cc]["v"] for cc in range(4)], axis=1)
                kv.append((kn, kr, vv))
            del res
            _dbg("kn", [kv[0][0][:, 0:1024], kv[1][0][:, 0:1024]])
            _dbg("kr", [kv[0][1], kv[1][1]])
            _dbg("vv", [kv[0][2][:, 0:8], kv[1][2][:, 0:8]])
        if l < n_a:
            zs = [(np.zeros((128, KC), np.float32),) * 2] * NCORES
            maps = [lru_inputs(inp, l, r, xfm, mods[r // 4][l], zs, False) for r in range(NCORES)]
            res = _run(build_lru(), maps)
            sums = [(res[r]["sA"], res[r]["sB"]) for r in range(NCORES)]
            maps = [lru_inputs(inp, l, r, xfm, mods[r // 4][l], sums, True) for r in range(NCORES)]
            res = _run(build_lru(), maps)
            xfm = [res[r]["xo"] for r in range(NCORES)]
            _dbg("xa%d" % l, xfm)
        else:
            j = l - n_a
            maps = []
            for r in range(NCORES):
                maps.append({"xT": xfm[r], "mod": np.ascontiguousarray(mods[r // 4][l][:, 0:24]), "ng": colT(inp["norm_mix_g"][l]),
                             "w_dq": np.ascontiguousarray(inp["mla_w_dq"][j]), "qg": colT(inp["mla_q_latent_g"][j]),
                             "w_uq": np.ascontiguousarray(inp["mla_w_uq"][j]), "pos": posc[r], "freq": freq32})
            res = _run(build_q(), maps)
            maps = []
            for r in range(NCORES):
                b, hg = r // 4, r % 4
                qf = np.concatenate([res[b * 4 + cc]["q"] for cc in range(4)], axis=2)
                kn, kr, vv = kv[b]
                kk = np.empty((96, HPC, SEQ), dtype=kn.dtype)
                for hh in range(HPC):
                    h = hg * HPC + hh
                    kk[0:64, hh, :] = kn[h * 64:(h + 1) * 64, :]
                    kk[64:96, hh, :] = kr
                vsel = vv[:, :, hg * 256:(hg + 1) * 256].reshape(128, NKT, HPC, 64).transpose(0, 2, 1, 3)
                maps.append({"q": np.ascontiguousarray(qf[:, hg * HPC:(hg + 1) * HPC, :]), "k": kk,
                             "v": np.ascontiguousarray(vsel)})
            res = _run(build_att(), maps)
            maps = []
            for r in range(NCORES):
                b, cc = r // 4, r % 4
                parts = [res[b * 4 + hg]["o"][:, :, cc * T:(cc + 1) * T] for hg in range(4)]
                at = np.concatenate([pp.transpose(1, 0, 2).reshape(HPC * 64, T) for pp in parts], axis=0)
                aT = np.ascontiguousarray(at.reshape(KC, 128, T).transpose(1, 0, 2))
                maps.append({"xT": xfm[r], "aT": aT, "mod": np.ascontiguousarray(mods[b][l][:, 0:24]),
                             "w_o": np.ascontiguousarray(inp["mla_w_o"][j])})
            _dbg("att%d" % l, [res[r]["o"][:, :, 0:1024] for r in range(NCORES)])
            res = _run(build_o(), maps)
            xfm = [res[r]["xo"] for r in range(NCORES)]
            _dbg("xa%d" % l, xfm)
        maps = [peer_inputs(inp, l, xfm[r], mods[r // 4][l], uT_l[l], vb_l[l]) for r in range(NCORES)]
        res = _run(build_peer(), maps)
        xfm = [res[r]["xo"] for r in range(NCORES)]
        _dbg("xb%d" % l, xfm)
    maps = [{"xT": xfm[r], "g": colT(inp["final_g"])} for r in range(NCORES)]
    res = _run(build_fin(), maps)
    out = np.empty((2, SEQ, D), np.float32)
    for r in range(NCORES):
        out[r // 4, (r % 4) * T:(r % 4 + 1) * T, :] = from_fm(res[r]["xo"])
    return out
```

```python
import numpy as np
from contextlib import ExitStack
import concourse.bass as bass
import concourse.mybir as mybir
from concourse.bass_utils import run_bass_kernel_spmd

F32 = mybir.dt.float32
BF16 = mybir.dt.bfloat16
I32 = mybir.dt.int32
U32 = mybir.dt.uint32
ALU = mybir.AluOpType
AF = mybir.ActivationFunctionType
AX = mybir.AxisListType

N_DMA_SEMS = 24


class Buf:
    __slots__ = ("name", "w", "r")

    def __init__(self, name):
        self.name = name
        self.w = None
        self.r = {}


class Tile:
    def __init__(self, t, name):
        self.t = t
        self.buf = Buf(name)

    def __getitem__(self, k):
        return self.t[k]


class Prog:
    def __init__(self, nc):
        self.nc = nc
        self.es = ExitStack()
        self.eng = {"pe": nc.tensor, "dve": nc.vector, "act": nc.scalar,
                    "pool": nc.gpsimd, "sp": nc.sync}
        self.sem = {k: self.es.enter_context(nc.semaphore("s_" + k)) for k in self.eng}
        self.cnt = {k: 0 for k in self.eng}
        self.seen = {k: {} for k in self.eng}
        self.dsem = [self.es.enter_context(nc.semaphore("d%d" % i)) for i in range(N_DMA_SEMS)]
        self.dcnt = [0] * N_DMA_SEMS
        self.dnext = 0
        self.nins = 0
        self.out_events = []

    def sb(self, name, shape, dt):
        t = self.es.enter_context(self.nc.sbuf_tensor(name, list(shape), dt))
        return Tile(t, name)

    def ps(self, name, shape, dt=F32):
        t = self.es.enter_context(self.nc.psum_tensor(name, list(shape), dt))
        return Tile(t, name)

    def _semh(self, key):
        return self.sem[key] if isinstance(key, str) else self.dsem[key[1]]

    def _wait(self, e, key, val):
        if self.seen[e].get(key, 0) >= val:
            return
        self.eng[e].wait_ge(self._semh(key), val)
        self.seen[e][key] = val

    def _deps(self, e, reads, writes):
        for b in reads:
            b = b.buf if isinstance(b, Tile) else b
            if b.w is not None:
                k, v = b.w
                if k == e and e == "pe":
                    continue
                self._wait(e, k, v)
        for b in writes:
            b = b.buf if isinstance(b, Tile) else b
            if b.w is not None:
                k, v = b.w
                if k != e:
                    self._wait(e, k, v)
            for k, v in b.r.items():
                if k != e:
                    self._wait(e, k, v)

    def _mark(self, ev, reads, writes):
        for b in reads:
            b = b.buf if isinstance(b, Tile) else b
            k, v = ev
            if b.r.get(k, 0) < v:
                b.r[k] = v
        for b in writes:
            b = b.buf if isinstance(b, Tile) else b
            b.w = ev
            b.r = {}

    def op(self, e, fn, reads=(), writes=()):
        self._deps(e, reads, writes)
        ins = fn(self.eng[e])
        self.cnt[e] += 1
        ins.then_inc(self.sem[e], 1)
        self._mark((e, self.cnt[e]), reads, writes)
        self.nins += 1
        return ins

    def dma(self, e, out, in_, reads=(), writes=(), is_output=False, **kw):
        i = self.dnext
        self.dnext = (self.dnext + 1) % N_DMA_SEMS
        key = ("dma", i)
        if self.dcnt[i] > 0:
            self._wait(e, key, self.dcnt[i])
        self._deps(e, reads, writes)
        ins = self.eng[e].dma_start(out=out, in_=in_, **kw)
        self.dcnt[i] += 16
        ins.then_inc(self.dsem[i], 16)
        ev = (key, self.dcnt[i])
        self._mark(ev, reads, writes)
        if is_output:
            self.out_events.append(ev)
        self.nins += 1
        return ins

    def finish(self):
        for k, v in self.out_events:
            self._wait("sp", k, v)
        for i in range(N_DMA_SEMS):
            if self.dcnt[i] > 0:
                self._wait("sp", ("dma", i), self.dcnt[i])
        self.es.close()


D = 1024
KC = 8
T = 2048
BLK = 512
NBLK = T // BLK
SEQ = 8192
EPS = 1e-6


def ext_in(nc, name, shape, dt=F32):
    return nc.dram_tensor(name, list(shape), dt, kind="ExternalInput").ap()


def ext_out(nc, name, shape, dt=F32):
    return nc.dram_tensor(name, list(shape), dt, kind="ExternalOutput").ap()


class Common:
    def __init__(self, p, need_ident=False, need_stage=True):
        self.p = p
        self.ones = p.sb("ones_bf", [128, 128], BF16)
        p.op("pool", lambda e: e.memset(self.ones[:], 1.0), writes=[self.ones])
        self.stage = [p.sb("stage%d" % i, [128, 2048], F32) for i in range(2)] if need_stage else []
        self.sti = 0
        if need_ident:
            io = p.sb("io", [128, 128], F32)
            pid = p.sb("pid", [128, 1], F32)
            self.identf = p.sb("identf", [128, 128], F32)
            self.ident = p.sb("ident", [128, 128], BF16)
            p.op("pool", lambda e: e.iota(io[:], [[1, 128]], base=0, channel_multiplier=0,
                                          allow_small_or_imprecise_dtypes=True), writes=[io])
            p.op("pool", lambda e: e.iota(pid[:], [[0, 1]], base=0, channel_multiplier=1,
                                          allow_small_or_imprecise_dtypes=True), writes=[pid])
            p.op("dve", lambda e: e.tensor_scalar(self.identf[:], io[:], pid[:, 0:1], None, ALU.is_equal),
                 reads=[io, pid], writes=[self.identf])
            p.op("dve", lambda e: e.tensor_copy(self.ident[:], self.identf[:]),
                 reads=[self.identf], writes=[self.ident])

    def next_stage(self):
        s = self.stage[self.sti % 2]
        self.sti += 1
        return s


def load_cast(p, cm, dram2d, dst_ap_fn, nk, M, cast_eng="pool", dst=None):
    for k in range(nk):
        st = cm.next_stage()
        p.dma("sp", st[:, 0:M], dram2d[k * 128:(k + 1) * 128, :], writes=[st])
        p.op(cast_eng, lambda e: e.tensor_copy(dst_ap_fn(k), st[:, 0:M]), reads=[st], writes=[dst])


def small_in(p, name, dram, shape, dt=F32):
    t = p.sb(name, shape, dt)
    p.dma("sp", t[:], dram, writes=[t])
    return t


def norm_block(p, cm, xin, nk, n, Acol, Bcol, out_fn, outbuf, wk, divisor, ps, reads_extra=()):
    sq, srt, rstd, tmp = wk["sq"], wk["srt"], wk["rstd"], wk["tmp"]
    p.op("act", lambda e: e.activation(sq[:, 0:nk, 0:n], xin[:, 0:nk, 0:n], AF.Square),
         reads=[xin], writes=[sq])
    for k in range(nk):
        p.op("pe", lambda e: e.matmul(ps[:, 0:n], cm.ones[:], sq[:, k, 0:n], start=(k == 0), stop=(k == nk - 1)),
             reads=[sq, cm.ones], writes=[ps])
    p.op("act", lambda e: e.activation(srt[:, 0:n], ps[:, 0:n], AF.Sqrt, scale=1.0 / divisor, bias=wk["eps"][:, 0:1]),
         reads=[ps, wk["eps"]], writes=[srt])
    p.op("dve", lambda e: e.reciprocal(rstd[:, 0:n], srt[:, 0:n]), reads=[srt], writes=[rstd])
    p.op("dve", lambda e: e.tensor_tensor(tmp[:, 0:nk, 0:n], xin[:, 0:nk, 0:n],
                                          rstd[:, 0:n].unsqueeze(1).to_broadcast([128, nk, n]), ALU.mult),
         reads=[xin, rstd], writes=[tmp])
    for k in range(nk):
        eng = "act" if k % 2 == 0 else "pool"
        if eng == "act":
            p.op("act", lambda e: e.activation(out_fn(k), tmp[:, k, 0:n], AF.Identity,
                                               scale=Acol[:, k:k + 1], bias=Bcol[:, k:k + 1]),
                 reads=[tmp, Acol, Bcol], writes=[outbuf])
        else:
            p.op("pool", lambda e: e.tensor_scalar(out_fn(k), tmp[:, k, 0:n], Acol[:, k:k + 1], Bcol[:, k:k + 1],
                                                   ALU.mult, ALU.add),
                 reads=[tmp, Acol, Bcol], writes=[outbuf])


def norm_work(p, nk=KC, n=BLK):
    wk = {
        "sq": p.sb("n_sq", [128, nk, n], BF16),
        "srt": p.sb("n_srt", [128, n], F32),
        "rstd": p.sb("n_rstd", [128, n], F32),
        "tmp": p.sb("n_tmp", [128, nk, n], F32),
        "eps": p.sb("n_eps", [128, 1], F32),
    }
    p.op("pool", lambda e: e.memset(wk["eps"][:], EPS), writes=[wk["eps"]])
    return wk


def mod_cols(p, name, modt, ng, sh_i, sc_i):
    A = p.sb(name + "_A", [128, KC], F32)
    t = p.sb(name + "_t", [128, KC], F32)
    p.op("dve", lambda e: e.tensor_scalar(t[:], modt[:, sc_i * 8:sc_i * 8 + 8], 1.0, None, ALU.add),
         reads=[modt], writes=[t])
    p.op("dve", lambda e: e.tensor_tensor(A[:], t[:], ng[:], ALU.mult), reads=[t, ng], writes=[A])
    return A


def build_mod():
    nc = bass.Bass("TRN2", target_bir_lowering=False)
    W = ext_in(nc, "W", [D, 6144])
    cT = ext_in(nc, "cT", [128, KC])
    bT = ext_in(nc, "bT", [128, 48])
    out = ext_out(nc, "out", [128, 48])
    p = Prog(nc)
    c_sb = small_in(p, "c_sb", cT, [128, KC])
    b_sb = small_in(p, "b_sb", bT, [128, 48])
    sil = p.sb("sil", [128, KC], F32)
    p.op("act", lambda e: e.activation(sil[:], c_sb[:], AF.Silu), reads=[c_sb], writes=[sil])
    wt = [p.sb("wt%d" % i, [128, KC, 1024], F32) for i in range(2)]
    ps = p.ps("ps", [128, 512], F32)
    o_sb = p.sb("o_sb", [128, 48], F32)
    for g in range(6):
        w = wt[g % 2]
        for k in range(KC):
            p.dma("sp", w[:, k, :], W[k * 128:(k + 1) * 128, g * 1024:(g + 1) * 1024], writes=[w])
        for m in range(8):
            col = g * 8 + m
            for k in range(KC):
                p.op("pe", lambda e: e.matmul(ps[:, col:col + 1], w[:, k, m * 128:(m + 1) * 128], sil[:, k:k + 1],
                                              start=(k == 0), stop=(k == KC - 1)),
                     reads=[w, sil], writes=[ps])
    p.op("dve", lambda e: e.tensor_tensor(o_sb[:], ps[:, 0:48], b_sb[:], ALU.add), reads=[ps, b_sb], writes=[o_sb])
    p.dma("sp", out, o_sb[:], reads=[o_sb], is_output=True)
    p.finish()
    return nc


NE_CONV = 2048 * 4


def build_conv():
    nc = bass.Bass("TRN2", target_bir_lowering=False)
    U = ext_in(nc, "U", [NE_CONV, D])
    V = ext_in(nc, "V", [NE_CONV, D])
    uT = ext_out(nc, "uT", [128, KC, NE_CONV], BF16)
    vb = ext_out(nc, "vb", [NE_CONV, D], BF16)
    p = Prog(nc)
    cm = Common(p, need_ident=True)
    nch = NE_CONV // 128
    uin = [p.sb("uin%d" % i, [128, D], F32) for i in range(2)]
    vin = [p.sb("vin%d" % i, [128, D], F32) for i in range(2)]
    vo = [p.sb("vo%d" % i, [128, D], BF16) for i in range(2)]
    uo = [p.sb("uo%d" % i, [128, KC, 512], BF16) for i in range(2)]
    pst = [p.ps("pst%d" % i, [128, 4, 128], F32) for i in range(4)]
    for ch in range(nch):
        ui = uin[ch % 2]
        vi = vin[ch % 2]
        p.dma("sp", ui[:], U[ch * 128:(ch + 1) * 128, :], writes=[ui])
        p.dma("sp", vi[:], V[ch * 128:(ch + 1) * 128, :], writes=[vi])
        v_o = vo[ch % 2]
        p.op("pool", lambda e: e.tensor_copy(v_o[:], vi[:]), reads=[vi], writes=[v_o])
        p.dma("pool", vb[ch * 128:(ch + 1) * 128, :], v_o[:], reads=[v_o], is_output=True)
        u_o = uo[(ch // 4) % 2]
        sub = ch % 4
        for half in range(2):
            pt = pst[(ch * 2 + half) % 4]
            for kk in range(4):
                k = half * 4 + kk
                p.op("pe", lambda e: e.transpose(pt[:, kk, :], ui[:, k * 128:(k + 1) * 128], cm.identf[:]),
                     reads=[ui, cm.identf], writes=[pt])
            eng = "act" if half == 0 else "dve"
            if eng == "act":
                p.op("act", lambda e: e.activation(u_o[:, half * 4:half * 4 + 4, sub * 128:(sub + 1) * 128], pt[:], AF.Copy),
                     reads=[pt], writes=[u_o])
            else:
                p.op("dve", lambda e: e.tensor_copy(u_o[:, half * 4:half * 4 + 4, sub * 128:(sub + 1) * 128], pt[:]),
                     reads=[pt], writes=[u_o])
        if sub == 3:
            g = ch // 4
            p.dma("sp", uT[:, :, g * 512:(g + 1) * 512], u_o[:], reads=[u_o], is_output=True)
    p.finish()
    return nc


def build_lru():
    nc = bass.Bass("TRN2", target_bir_lowering=False)
    xT = ext_in(nc, "xT", [128, KC, T])
    xh = ext_in(nc, "xh", [128, KC, 4])
    flag = ext_in(nc, "flag", [128, 1])
    modd = ext_in(nc, "mod", [128, 24])
    ngd = ext_in(nc, "ng", [128, KC])
    w_in = ext_in(nc, "w_in", [D, 2 * D])
    cwd = ext_in(nc, "cw", [128, KC, 4])
    cbd = ext_in(nc, "cb", [128, KC])
    wad = ext_in(nc, "wa", [4, 256, 256])
    bad = ext_in(nc, "ba", [128, KC])
    wxd = ext_in(nc, "wx", [4, 256, 256])
    bxd = ext_in(nc, "bx", [128, KC])
    lamd = ext_in(nc, "lam", [128, KC])
    w_out = ext_in(nc, "w_out", [D, D])
    sumA = ext_in(nc, "sumA", [128, 4, KC])
    sumB = ext_in(nc, "sumB", [128, 4, KC])
    cmaskd = ext_in(nc, "cmask", [128, 4])
    xo = ext_out(nc, "xo", [128, KC, T])
    sAo = ext_out(nc, "sA", [128, KC])
    sBo = ext_out(nc, "sB", [128, KC])

    p = Prog(nc)
    cm = Common(p)
    wk = norm_work(p)
    flg = small_in(p, "flg", flag, [128, 1])
    modt = small_in(p, "modt", modd, [128, 24])
    ng = small_in(p, "ngt", ngd, [128, KC])
    cw = small_in(p, "cwt", cwd, [128, KC, 4])
    cb = small_in(p, "cbt", cbd, [128, KC])
    ba = small_in(p, "bat", bad, [128, KC])
    bx = small_in(p, "bxt", bxd, [128, KC])
    lam = small_in(p, "lamt", lamd, [128, KC])
    sA_in = small_in(p, "sA_in", sumA, [128, 4, KC])
    sB_in = small_in(p, "sB_in", sumB, [128, 4, KC])
    cmask = small_in(p, "cmaskt", cmaskd, [128, 4])

    A1 = mod_cols(p, "m1", modt, ng, 0, 1)
    e1 = p.sb("e1", [128, KC], F32)
    scol = p.sb("scol", [128, KC], F32)
    scol2 = p.sb("scol2", [128, KC], F32)
    p.op("act", lambda e: e.activation(e1[:], lam[:], AF.Exp, scale=-1.0), reads=[lam], writes=[e1])
    p.op("dve", lambda e: e.tensor_scalar(e1[:], e1[:], 1.0, None, ALU.add), reads=[e1], writes=[e1])
    p.op("act", lambda e: e.activation(e1[:], e1[:], AF.Ln), reads=[e1], writes=[e1])
    p.op("dve", lambda e: e.tensor_scalar(scol[:], e1[:], -8.0, None, ALU.mult), reads=[e1], writes=[scol])
    p.op("dve", lambda e: e.tensor_scalar(scol2[:], e1[:], -16.0, None, ALU.mult), reads=[e1], writes=[scol2])

    carry = p.sb("carry", [128, KC], F32)
    cprod = p.sb("cprod", [128, KC], F32)
    t1 = p.sb("t1", [128, KC], F32)
    p.op("dve", lambda e: e.memset(carry[:], 0.0), writes=[carry])
    p.op("dve", lambda e: e.memset(cprod[:], 1.0), writes=[cprod])
    for j in range(4):
        p.op("dve", lambda e: e.tensor_tensor(t1[:], sA_in[:, j, :], carry[:], ALU.mult), reads=[sA_in, carry], writes=[t1])
        p.op("dve", lambda e: e.tensor_tensor(t1[:], t1[:], sB_in[:, j, :], ALU.add), reads=[t1, sB_in], writes=[t1])
        p.op("dve", lambda e: e.tensor_tensor(t1[:], t1[:], carry[:], ALU.subtract), reads=[t1, carry], writes=[t1])
        p.op("dve", lambda e: e.scalar_tensor_tensor(carry[:], t1[:], cmask[:, j:j + 1], carry[:], ALU.mult, ALU.add),
             reads=[t1, cmask, carry], writes=[carry])

    win = p.sb("win", [128, KC, 2 * D], BF16)
    wout = p.sb("wout", [128, KC, D], BF16)
    wab = p.sb("wab", [128, 4, 2, 256], BF16)
    wxb = p.sb("wxb", [128, 4, 2, 256], BF16)
    load_cast(p, cm, w_in, lambda k: win[:, k, :], KC, 2 * D, dst=win)
    load_cast(p, cm, w_out, lambda k: wout[:, k, :], KC, D, dst=wout)
    for (src, dstt) in ((wad, wab), (wxd, wxb)):
        st = cm.next_stage()
        p.dma("sp", st[:, 0:2048].rearrange("p (h kk m) -> p h kk m", h=4, kk=2),
              src.rearrange("h (kk p) m -> p h kk m", p=128), writes=[st])
        p.op("pool", lambda e: e.tensor_copy(dstt[:].rearrange("p h kk m -> p (h kk m)"), st[:, 0:2048]),
             reads=[st], writes=[dstt])

    hT = p.sb("hT", [128, KC, BLK], BF16)
    yT = p.sb("yT", [128, KC, BLK], BF16)
    xblk = [p.sb("xblk%d" % i, [128, KC, BLK], F32) for i in range(2)]
    psn = p.ps("psn", [128, 512], F32)
    pmm = [p.ps("pmm%d" % i, [128, 512], F32) for i in range(4)]
    pmi = [0]

    def next_ps():
        t = pmm[pmi[0] % 4]
        pmi[0] += 1
        return t

    xhh = p.sb("xhh", [128, KC, 4], F32)
    hh = p.sb("hh", [128, KC, 4], BF16)
    tail = p.sb("tail", [128, KC, 4], F32)
    p.dma("sp", xhh[:], xh, writes=[xhh])
    norm_block(p, cm, xhh, KC, 4, A1, modt, lambda k: hh[:, k, :], hh, wk, float(D), psn)
    for j in range(KC):
        c0 = D + j * 128
        ps = next_ps()
        for k in range(KC):
            p.op("pe", lambda e: e.matmul(ps[:, 0:4], win[:, k, c0:c0 + 128], hh[:, k, :], start=(k == 0), stop=(k == KC - 1)),
                 reads=[win, hh], writes=[ps])
        p.op("dve", lambda e: e.tensor_scalar(tail[:, j, :], ps[:, 0:4], flg[:, 0:1], None, ALU.mult),
             reads=[ps, flg], writes=[tail])

    rec = p.sb("rec", [128, 2, 4 + BLK], F32)
    xc = p.sb("xc", [128, 2, BLK], F32)
    xcb = p.sb("xcb", [128, 2, BLK], BF16)
    rr = p.sb("rr", [128, 2, BLK], F32)
    ii = p.sb("ii", [128, 2, BLK], F32)
    aa = p.sb("aa", [128, 2, BLK], F32)
    a2 = p.sb("a2", [128, 2, BLK], F32)
    hs = p.sb("hs", [128, 2, BLK], F32)
    pscan = p.sb("pscan", [128, BLK], F32)
    zeros = p.sb("zeros", [128, BLK], F32)
    gt = p.sb("gt", [128, 2, BLK], F32)
    p.op("pool", lambda e: e.memset(zeros[:], 0.0), writes=[zeros])

    for b in range(NBLK):
        xb = xblk[b % 2]
        p.dma("sp", xb[:], xT[:, :, b * BLK:(b + 1) * BLK], writes=[xb])
        norm_block(p, cm, xb, KC, BLK, A1, modt, lambda k: hT[:, k, :], hT, wk, float(D), psn)
        for hd in range(4):
            for jj in range(2):
                j = 2 * hd + jj
                c0 = D + j * 128
                p.op("pool", lambda e: e.tensor_copy(rec[:, jj, 0:4], tail[:, j, :]), reads=[tail], writes=[rec])
                ps = next_ps()
                for k in range(KC):
                    p.op("pe", lambda e: e.matmul(ps[:], win[:, k, c0:c0 + 128], hT[:, k, :],
                                                  start=(k == 0), stop=(k == KC - 1)),
                         reads=[win, hT], writes=[ps])
                p.op("act", lambda e: e.activation(rec[:, jj, 4:4 + BLK], ps[:], AF.Copy),
                     reads=[ps], writes=[rec])
                p.op("pool", lambda e: e.tensor_copy(tail[:, j, :], rec[:, jj, BLK:BLK + 4]), reads=[rec], writes=[tail])
            o = 4
            for jj in range(2):
                j = 2 * hd + jj
                p.op("dve", lambda e: e.tensor_scalar(xc[:, jj, :], rec[:, jj, o:o + BLK], cw[:, j, 3:4], cb[:, j:j + 1],
                                                      ALU.mult, ALU.add), reads=[rec, cw, cb], writes=[xc])
                for s in range(1, 4):
                    p.op("dve", lambda e: e.scalar_tensor_tensor(xc[:, jj, :], rec[:, jj, o - s:o - s + BLK],
                                                                 cw[:, j, 3 - s:4 - s], xc[:, jj, :], ALU.mult, ALU.add),
                         reads=[rec, cw, xc], writes=[xc])
            p.op("act", lambda e: e.activation(xcb[:], xc[:], AF.Copy), reads=[xc], writes=[xcb])
            for (wg, bg, dstg) in ((wab, ba, rr), (wxb, bx, ii)):
                for jj in range(2):
                    j = 2 * hd + jj
                    ps = next_ps()
                    for kk in range(2):
                        p.op("pe", lambda e: e.matmul(ps[:], wg[:, hd, kk, jj * 128:(jj + 1) * 128], xcb[:, kk, :],
                                                      start=(kk == 0), stop=(kk == 1)), reads=[wg, xcb], writes=[ps])
                    p.op("act", lambda e: e.activation(dstg[:, jj, :], ps[:], AF.Sigmoid, bias=bg[:, j:j + 1]),
                         reads=[ps, bg], writes=[dstg])
            for jj in range(2):
                j = 2 * hd + jj
                p.op("act", lambda e: e.activation(aa[:, jj, :], rr[:, jj, :], AF.Exp, scale=scol[:, j:j + 1]),
                     reads=[rr, scol], writes=[aa])
                p.op("act", lambda e: e.activation(a2[:, jj, :], rr[:, jj, :], AF.Exp, scale=scol2[:, j:j + 1]),
                     reads=[rr, scol2], writes=[a2])
            p.op("dve", lambda e: e.tensor_scalar(a2[:], a2[:], -1.0, 1.0, ALU.mult, ALU.add), reads=[a2], writes=[a2])
            p.op("act", lambda e: e.activation(a2[:], a2[:], AF.Sqrt), reads=[a2], writes=[a2])
            p.op("dve", lambda e: e.tensor_tensor(ii[:], ii[:], xc[:], ALU.mult), reads=[ii, xc], writes=[ii])
            p.op("dve", lambda e: e.tensor_tensor(ii[:], ii[:], a2[:], ALU.mult), reads=[ii, a2], writes=[ii])
            for jj in range(2):
                j = 2 * hd + jj
                p.op("dve", lambda e: e.tensor_tensor_scan(hs[:, jj, :], aa[:, jj, :], ii[:, jj, :], carry[:, j:j + 1],
                                                           ALU.mult, ALU.add), reads=[aa, ii, carry], writes=[hs])
                p.op("dve", lambda e: e.tensor_copy(carry[:, j:j + 1], hs[:, jj, BLK - 1:BLK]), reads=[hs], writes=[carry])
                p.op("dve", lambda e: e.tensor_tensor_scan(pscan[:], aa[:, jj, :], zeros[:], cprod[:, j:j + 1],
                                                           ALU.mult, ALU.add), reads=[aa, zeros, cprod], writes=[pscan])
                p.op("dve", lambda e: e.tensor_copy(cprod[:, j:j + 1], pscan[:, BLK - 1:BLK]), reads=[pscan], writes=[cprod])
            for jj in range(2):
                j = 2 * hd + jj
                ps = next_ps()
                for k in range(KC):
                    p.op("pe", lambda e: e.matmul(ps[:], win[:, k, j * 128:(j + 1) * 128], hT[:, k, :],
                                                  start=(k == 0), stop=(k == KC - 1)), reads=[win, hT], writes=[ps])
                p.op("act", lambda e: e.activation(gt[:, jj, :], ps[:], AF.Gelu), reads=[ps], writes=[gt])
            p.op("pool", lambda e: e.tensor_tensor(yT[:, 2 * hd:2 * hd + 2, :], gt[:], hs[:], ALU.mult),
                 reads=[gt, hs], writes=[yT])
        for m in range(KC):
            ps = next_ps()
            for k in range(KC):
                p.op("pe", lambda e: e.matmul(ps[:], wout[:, k, m * 128:(m + 1) * 128], yT[:, k, :],
                                              start=(k == 0), stop=(k == KC - 1)), reads=[wout, yT], writes=[ps])
            p.op("dve", lambda e: e.scalar_tensor_tensor(xb[:, m, :], ps[:], modt[:, 16 + m:17 + m], xb[:, m, :],
                                                         ALU.mult, ALU.add), reads=[ps, modt, xb], writes=[xb])
        p.dma("sp", xo[:, :, b * BLK:(b + 1) * BLK], xb[:], reads=[xb], is_output=True)
    p.dma("sp", sAo, cprod[:], reads=[cprod], is_output=True)
    p.dma("sp", sBo, carry[:], reads=[carry], is_output=True)
    p.finish()
    return nc


def colT(v, nk=None):
    v = np.asarray(v)
    nk = v.shape[0] // 128
    return np.ascontiguousarray(v.reshape(nk, 128).T)


def to_fm(xtok):
    Tn, Dn = xtok.shape
    return np.ascontiguousarray(xtok.T.reshape(Dn // 128, 128, Tn).transpose(1, 0, 2))


def from_fm(xfm):
    P, nk, Tn = xfm.shape
    return np.ascontiguousarray(xfm.transpose(1, 0, 2).reshape(nk * 128, Tn).T)


def lru_inputs(inp, l, r, xfm_all, mod_r, sums, use_mask):
    b, c = r // 4, r % 4
    xh = np.zeros((128, KC, 4), np.float32)
    if c > 0:
        xh[:, :, 1:4] = xfm_all[r - 1][:, :, T - 3:T]
    sumA = np.zeros((128, 4, KC), np.float32)
    sumB = np.zeros((128, 4, KC), np.float32)
    cmask = np.zeros((128, 4), np.float32)
    if use_mask:
        for j in range(4):
            sumA[:, j, :] = sums[b * 4 + j][0]
            sumB[:, j, :] = sums[b * 4 + j][1]
            if j < c:
                cmask[:, j] = 1.0
    return {
        "xT": xfm_all[r], "xh": xh,
        "flag": np.full((128, 1), 1.0 if c > 0 else 0.0, np.float32),
        "mod": np.ascontiguousarray(mod_r[:, 0:24]),
        "ng": colT(inp["norm_mix_g"][l]),
        "w_in": np.ascontiguousarray(inp["lru_w_in"][l]),
        "cw": np.ascontiguousarray(inp["lru_conv_w"][l].reshape(4, KC, 128).transpose(2, 1, 0)),
        "cb": colT(inp["lru_conv_b"][l]),
        "wa": np.ascontiguousarray(inp["lru_wa"][l]), "ba": colT(inp["lru_ba"][l]),
        "wx": np.ascontiguousarray(inp["lru_wx"][l]), "bx": colT(inp["lru_bx"][l]),
        "lam": colT(inp["lru_lambda"][l]),
        "w_out": np.ascontiguousarray(inp["lru_w_out"][l]),
        "sumA": sumA, "sumB": sumB, "cmask": cmask,
    }


NEXP = 16384
PAIR = 256
NPAIR = T // PAIR
THR_DELTA = 2e-5


def build_peer(npair=NPAIR):
    nc = bass.Bass("TRN2", target_bir_lowering=False)
    xT = ext_in(nc, "xT", [128, KC, T])
    modd = ext_in(nc, "mod", [128, 24])
    ngd = ext_in(nc, "ng", [128, KC])
    w_q = ext_in(nc, "w_q", [D, 2 * D])
    skd = ext_in(nc, "skT", [128, 16, 128])
    uT = ext_in(nc, "uT", [128, KC, NEXP], BF16)
    vb = ext_in(nc, "vb", [NEXP, D], BF16)
    xo = ext_out(nc, "xo", [128, KC, T])

    p = Prog(nc)
    cm = Common(p, need_ident=True, need_stage=False)
    wk = norm_work(p, KC, PAIR)
    modt = small_in(p, "modt", modd, [128, 24])
    ng = small_in(p, "ngt", ngd, [128, KC])
    A2 = mod_cols(p, "m2", modt, ng, 0, 1)
    skT = small_in(p, "skTf", skd, [128, 16, 128])
    wq_v = w_q.rearrange("(k q) m -> q k m", q=128)
    wqj = [p.sb("wqj%d" % i, [128, KC, 128], F32) for i in range(2)]
    hTf = p.sb("hTf", [128, KC, PAIR], F32)

    xp = p.sb("xp", [128, KC, PAIR], F32)
    hTp = p.sb("hTp", [128, KC, PAIR], BF16)
    qT = p.sb("qT", [128, 16, PAIR], F32)
    S_sb = [p.sb("S_sb%d" % i, [128, 16, 128], F32) for i in range(2)]
    th = [p.sb("th%d" % i, [128, 8, 128], F32) for i in range(2)]
    P1 = [p.sb("P1_%d" % i, [128, 8, 128], BF16) for i in range(2)]
    P2 = [p.sb("P2_%d" % i, [128, 8, 128], BF16) for i in range(2)]
    v16 = p.sb("v16", [128, 16, 16], F32)
    work = p.sb("work", [128, 128], F32)
    cand = p.sb("cand", [128, 8, 16, 16], F32)
    work2 = p.sb("work2", [128, 16, 16], F32)
    cv = p.sb("cv", [128, 8, 16], F32)
    thr = p.sb("thr", [128, 8], F32)
    dd = p.sb("dd", [128, 8, 16], F32)
    Z = p.sb("Z", [128, 8], F32)
    rZ = p.sb("rZ", [128, 8], F32)
    Psub = p.sb("Psub", [128, 16, 128], F32)
    G = [p.sb("G%d" % i, [128, 16, 128], BF16) for i in range(2)]
    mks = [p.sb("mk%d" % i, [128, 16, 128], BF16) for i in range(2)]
    Ets = [p.sb("Et%d" % i, [128, 16, 128], BF16) for i in range(2)]
    MEs = [p.sb("ME%d" % i, [128, 16, 128], BF16) for i in range(2)]
    uTg = [p.sb("uTg%d" % i, [128, KC, 512], BF16) for i in range(2)]
    vbg = [p.sb("vbg%d" % i, [128, 4, D], BF16) for i in range(2)]
    gl = [p.sb("gl%d" % i, [128, 512], BF16) for i in range(2)]
    Wt = [p.sb("Wt%d" % i, [128, 512], BF16) for i in range(2)]
    WT_sb = [p.sb("WT_sb%d" % i, [128, 2, 4, 128], BF16) for i in range(2)]
    ao = p.sb("ao", [128, D], F32)
    tmpo = p.sb("tmpo", [128, 4, 128], F32)

    acc = [p.ps("acc%d" % i, [128, D], F32) for i in range(2)]
    Sps = [p.ps("Sps%d" % i, [128, 512], F32) for i in range(2)]
    WTps = [p.ps("WTps%d" % i, [128, 2, 4, 128], BF16) for i in range(2)]

    ngrp = NEXP // 512
    SLAB = 16
    for pr in range(npair):
        c0 = pr * PAIR
        p.dma("sp", xp[:], xT[:, :, c0:c0 + PAIR], writes=[xp])
        norm_block(p, cm, xp, KC, PAIR, A2, modt, lambda k: hTf[:, k, :], hTf, wk, float(D), Sps[0])
        p.op("pool", lambda e: e.tensor_copy(hTp[:], hTf[:]), reads=[hTf], writes=[hTp])
        for j in range(16):
            ps = Sps[j % 2]
            wj = wqj[j % 2]
            p.dma("sp", wj[:], wq_v[:, :, j * 128:(j + 1) * 128], writes=[wj])
            for k in range(KC):
                p.op("pe", lambda e: e.matmul(ps[:, 0:PAIR], wj[:, k, :], hTf[:, k, :],
                                              start=(k == 0), stop=(k == KC - 1)), reads=[wj, hTf], writes=[ps])
            if j % 2 == 0:
                p.op("act", lambda e: e.activation(qT[:, j, :], ps[:, 0:PAIR], AF.Copy), reads=[ps], writes=[qT])
            else:
                p.op("dve", lambda e: e.tensor_copy(qT[:, j, :], ps[:, 0:PAIR]), reads=[ps], writes=[qT])
        for tt in range(2):
            S = S_sb[tt]
            for grp in range(4):
                ps = Sps[grp % 2]
                for jj in range(4):
                    j = grp * 4 + jj
                    p.op("pe", lambda e: e.matmul(ps[:, jj * 128:(jj + 1) * 128], qT[:, j, tt * 128:(tt + 1) * 128], skT[:, j, :],
                                                  start=True, stop=True), reads=[qT, skT], writes=[ps])
                p.op("act", lambda e: e.activation(S[:, grp * 4:grp * 4 + 4, :].rearrange("p j n -> p (j n)"), ps[:], AF.Copy),
                     reads=[ps], writes=[S])
            for j in range(16):
                p.op("dve", lambda e: e.max(v16[:, j, 0:8], S[:, j, :]), reads=[S], writes=[v16])
                p.op("dve", lambda e: e.match_replace(work[:], v16[:, j, 0:8], S[:, j, :], -1e30), reads=[S, v16], writes=[work])
                p.op("dve", lambda e: e.max(v16[:, j, 8:16], work[:]), reads=[work], writes=[v16])
            v4 = v16[:].rearrange("p (h two) k -> p h two k", two=2)
            p.op("dve", lambda e: e.tensor_tensor(cand[:], v4[:, :, 0, :].unsqueeze(3).to_broadcast([128, 8, 16, 16]),
                                                  v4[:, :, 1, :].unsqueeze(2).to_broadcast([128, 8, 16, 16]), ALU.add),
                 reads=[v16], writes=[cand])
            for h in range(8):
                p.op("dve", lambda e: e.max(cv[:, h, 0:8], cand[:, h]), reads=[cand], writes=[cv])
                p.op("dve", lambda e: e.match_replace(work2[:], cv[:, h, 0:8], cand[:, h], -1e30), reads=[cand, cv], writes=[work2])
                p.op("dve", lambda e: e.max(cv[:, h, 8:16], work2[:]), reads=[work2], writes=[cv])
            p.op("dve", lambda e: e.tensor_scalar(thr[:], cv[:, :, 15], -THR_DELTA, None, ALU.add), reads=[cv], writes=[thr])
            p.op("dve", lambda e: e.tensor_tensor(dd[:], cv[:], cv[:, :, 0:1].to_broadcast([128, 8, 16]), ALU.subtract),
                 reads=[cv], writes=[dd])
            p.op("act", lambda e: e.activation(dd[:], dd[:], AF.Exp), reads=[dd], writes=[dd])
            p.op("dve", lambda e: e.tensor_reduce(Z[:], dd[:], AX.X, ALU.add), reads=[dd], writes=[Z])
            p.op("dve", lambda e: e.reciprocal(rZ[:], Z[:]), reads=[Z], writes=[rZ])
            p.op("dve", lambda e: e.tensor_tensor(Psub[:], S[:], v16[:, :, 0:1].to_broadcast([128, 16, 128]), ALU.subtract),
                 reads=[S, v16], writes=[Psub])
            p.op("act", lambda e: e.activation(Psub[:], Psub[:], AF.Exp), reads=[Psub], writes=[Psub])
            Pv = Psub[:].rearrange("p (h two) n -> p h two n", two=2)
            Sv = S[:].rearrange("p (h two) n -> p h two n", two=2)
            p.op("dve", lambda e: e.tensor_tensor(P1[tt][:], Pv[:, :, 0, :], rZ[:].unsqueeze(2).to_broadcast([128, 8, 128]), ALU.mult),
                 reads=[Psub, rZ], writes=[P1[tt]])
            p.op("pool", lambda e: e.tensor_copy(P2[tt][:], Pv[:, :, 1, :]), reads=[Psub], writes=[P2[tt]])
            p.op("dve", lambda e: e.tensor_tensor(th[tt][:], thr[:].unsqueeze(2).to_broadcast([128, 8, 128]), Sv[:, :, 0, :], ALU.subtract),
                 reads=[thr, S], writes=[th[tt]])

        def load_grp(g):
            e0 = g * 512
            u = uTg[g % 2]
            v = vbg[g % 2]
            p.dma("sp", u[:], uT[:, :, e0:e0 + 512], writes=[u])
            p.dma("sp", v[:], vb[e0:e0 + 512, :].rearrange("(c q) d -> q c d", q=128), writes=[v])

        load_grp(0)
        for g in range(ngrp):
            if g % 4 == 0:
                slab = g // 4
                e1a = slab * SLAB
                for tt in range(2):
                    Sv = S_sb[tt][:].rearrange("p (h two) n -> p h two n", two=2)
                    for h in range(8):
                        mk, Et, ME = mks[h % 2], Ets[h % 2], MEs[h % 2]
                        p.op("dve", lambda e: e.tensor_tensor(mk[:], Sv[:, h, 1, :].unsqueeze(1).to_broadcast([128, SLAB, 128]),
                                                              th[tt][:, h, e1a:e1a + SLAB].unsqueeze(2).to_broadcast([128, SLAB, 128]),
                                                              ALU.is_ge), reads=[S_sb[tt], th[tt]], writes=[mk])
                        p.op("dve", lambda e: e.tensor_tensor(Et[:], P2[tt][:, h, :].unsqueeze(1).to_broadcast([128, SLAB, 128]),
                                                               P1[tt][:, h, e1a:e1a + SLAB].unsqueeze(2).to_broadcast([128, SLAB, 128]),
                                                               ALU.mult), reads=[P1[tt], P2[tt]], writes=[Et])
                        if h == 0:
                            p.op("dve", lambda e: e.tensor_tensor(G[tt][:], mk[:], Et[:], ALU.mult), reads=[mk, Et], writes=[G[tt]])
                        else:
                            p.op("dve", lambda e: e.tensor_tensor(ME[:], mk[:], Et[:], ALU.mult), reads=[mk, Et], writes=[ME])
                            p.op("dve", lambda e: e.tensor_tensor(G[tt][:], G[tt][:], ME[:], ALU.add), reads=[G[tt], ME], writes=[G[tt]])
            if g + 1 < ngrp:
                load_grp(g + 1)
            u = uTg[g % 2]
            v = vbg[g % 2]
            gi = g % 4
            wtp = WTps[g % 2]
            wts = WT_sb[g % 2]
            for tt in range(2):
                ps = Sps[tt]
                for k in range(KC):
                    p.op("pe", lambda e: e.matmul(ps[:], hTp[:, k, tt * 128:(tt + 1) * 128], u[:, k, :],
                                                  start=(k == 0), stop=(k == KC - 1)), reads=[hTp, u], writes=[ps])
                p.op("act", lambda e: e.activation(gl[tt][:], ps[:], AF.Gelu), reads=[ps], writes=[gl[tt]])
                p.op("dve", lambda e: e.tensor_tensor(Wt[tt][:], gl[tt][:],
                                                      G[tt][:, gi * 4:gi * 4 + 4, :].rearrange("p a n -> p (a n)"), ALU.mult),
                     reads=[gl[tt], G[tt]], writes=[Wt[tt]])
                for c in range(4):
                    p.op("pe", lambda e: e.transpose(wtp[:, tt, c, :], Wt[tt][:, c * 128:(c + 1) * 128], cm.ident[:]),
                         reads=[Wt[tt], cm.ident], writes=[wtp])
            p.op("act", lambda e: e.activation(wts[:].rearrange("p a c n -> p (a c n)"), wtp[:].rearrange("p a c n -> p (a c n)"), AF.Copy),
                 reads=[wtp], writes=[wts])
            for tt in range(2):
                for c in range(4):
                    for half in range(2):
                        p.op("pe", lambda e: e.matmul(acc[tt][:, half * 512:(half + 1) * 512], wts[:, tt, c, :], v[:, c, half * 512:(half + 1) * 512],
                                                      start=(g == 0 and c == 0), stop=(g == ngrp - 1 and c == 3)),
                             reads=[wts, v], writes=[acc[tt]])
        for tt in range(2):
            p.op("act", lambda e: e.activation(ao[:], acc[tt][:], AF.Copy), reads=[acc[tt]], writes=[ao])
            for half in range(2):
                ps = Sps[half]
                for kk in range(4):
                    k = half * 4 + kk
                    p.op("pe", lambda e: e.transpose(ps[:, kk * 128:(kk + 1) * 128], ao[:, k * 128:(k + 1) * 128], cm.identf[:]),
                         reads=[ao, cm.identf], writes=[ps])
                p.op("dve", lambda e: e.tensor_tensor(tmpo[:], ps[:].rearrange("p (a n) -> p a n", a=4),
                                                      modt[:, 16 + half * 4:20 + half * 4].unsqueeze(2).to_broadcast([128, 4, 128]), ALU.mult),
                     reads=[ps, modt], writes=[tmpo])
                p.op("pool", lambda e: e.tensor_tensor(xp[:, half * 4:half * 4 + 4, tt * 128:(tt + 1) * 128], tmpo[:],
                                                       xp[:, half * 4:half * 4 + 4, tt * 128:(tt + 1) * 128], ALU.add),
                     reads=[tmpo, xp], writes=[xp])
        p.dma("sp", xo[:, :, c0:c0 + PAIR], xp[:], reads=[xp], is_output=True)
    p.finish()
    print("peer nins", p.nins)
    return nc


def peer_inputs(inp, l, xfm, mod_r, uT_l, vb_l):
    sk = inp["peer_sub_keys"][l]
    skT = np.ascontiguousarray(sk.reshape(16, 128, 128).transpose(2, 0, 1))
    return {
        "xT": xfm, "mod": np.ascontiguousarray(mod_r[:, 24:48]),
        "ng": colT(inp["norm_ffn_g"][l]),
        "w_q": np.ascontiguousarray(inp["peer_w_q"][l]),
        "skT": skT, "uT": uT_l, "vb": vb_l,
    }


TWO_PI = 6.283185307179586


def rope_tables(p, posd, freqd, n, scale, name, pb=0):
    P_ = pb + 32
    sl = slice(pb, pb + 32)
    posi = p.sb(name + "_posi", [P_, n], I32)
    p.dma("sp", posi[sl, :], posd.partition_broadcast(32), writes=[posi])
    fr = p.sb(name + "_fr", [P_, 1], F32)
    p.dma("sp", fr[sl, :], freqd, writes=[fr])
    negpi = p.sb(name + "_negpi", [P_, 1], F32)
    p.op("pool", lambda e: e.memset(negpi[sl, :], -3.141592653589793), writes=[negpi])
    ang = p.sb(name + "_ang", [P_, n], F32)
    p.op("dve", lambda e: e.tensor_copy(ang[sl, :], posi[sl, :]), reads=[posi], writes=[ang])
    p.op("dve", lambda e: e.tensor_scalar(ang[sl, :], ang[sl, :], fr[sl, 0:1], None, ALU.mult), reads=[ang, fr], writes=[ang])
    outs = []
    ki = posi
    kf = p.sb(name + "_kf", [P_, n], F32)
    for (nm, off) in (("sin", 0.5), ("cos", 0.75)):
        u = p.sb(name + "_u" + nm, [P_, n], F32)
        p.op("dve", lambda e: e.tensor_scalar(u[sl, :], ang[sl, :], 1.0 / TWO_PI, off, ALU.mult, ALU.add), reads=[ang], writes=[u])
        p.op("dve", lambda e: e.tensor_copy(ki[sl, :], u[sl, :]), reads=[u], writes=[ki])
        p.op("dve", lambda e: e.tensor_copy(kf[sl, :], ki[sl, :]), reads=[ki], writes=[kf])
        p.op("dve", lambda e: e.tensor_tensor(u[sl, :], u[sl, :], kf[sl, :], ALU.subtract), reads=[u, kf], writes=[u])
        p.op("dve", lambda e: e.tensor_scalar(kf[sl, :], u[sl, :], 0.0, None, ALU.is_lt), reads=[u], writes=[kf])
        p.op("dve", lambda e: e.tensor_tensor(u[sl, :], u[sl, :], kf[sl, :], ALU.add), reads=[u, kf], writes=[u])
        p.op("dve", lambda e: e.tensor_scalar(u[sl, :], u[sl, :], 0.9999999, None, ALU.min), reads=[u], writes=[u])
        p.op("act", lambda e: e.activation(u[sl, :], u[sl, :], AF.Sin, scale=TWO_PI, bias=negpi[sl, 0:1]), reads=[u, negpi], writes=[u])
        if scale != 1.0:
            p.op("dve", lambda e: e.tensor_scalar(u[sl, :], u[sl, :], float(scale), None, ALU.mult), reads=[u], writes=[u])
        outs.append(u)
    return outs[1], outs[0]


def build_kv():
    nc = bass.Bass("TRN2", target_bir_lowering=False)
    xT = ext_in(nc, "xT", [128, KC, T])
    modd = ext_in(nc, "mod", [128, 16])
    ngd = ext_in(nc, "ng", [128, KC])
    w_dkv = ext_in(nc, "w_dkv", [D, 256])
    w_kr = ext_in(nc, "w_kr", [D, 32])
    lgd = ext_in(nc, "lg", [128, 2])
    w_uk = ext_in(nc, "w_uk", [256, D])
    w_uv = ext_in(nc, "w_uv", [256, D])
    posd = ext_in(nc, "pos", [1, T], I32)
    freqd = ext_in(nc, "freq", [32, 1])
    kTo = ext_out(nc, "kT", [128, KC, T], BF16)
    kro = ext_out(nc, "krT", [32, T], BF16)
    vo = ext_out(nc, "v", [128, T // 128, D], BF16)

    p = Prog(nc)
    cm = Common(p)
    wk = norm_work(p)
    modt = small_in(p, "modt", modd, [128, 16])
    ng = small_in(p, "ngt", ngd, [128, KC])
    lg = small_in(p, "lgt", lgd, [128, 2])
    zer = p.sb("zer", [128, KC], F32)
    p.op("pool", lambda e: e.memset(zer[:], 0.0), writes=[zer])
    A = mod_cols(p, "mkv", modt, ng, 0, 1)
    cos, sin = rope_tables(p, posd, freqd, T, 1.0, "rk")
    wdkv = p.sb("wdkv", [128, KC, 256], BF16)
    wkr = p.sb("wkr", [128, KC, 32], BF16)
    wsw = p.sb("wsw", [128, KC, 32], BF16)
    wuk = p.sb("wuk", [128, 2, D], BF16)
    wuv = p.sb("wuv", [128, 2, D], BF16)
    load_cast(p, cm, w_dkv, lambda k: wdkv[:, k, :], KC, 256, dst=wdkv)
    load_cast(p, cm, w_uk, lambda k: wuk[:, k, :], 2, D, dst=wuk)
    load_cast(p, cm, w_uv, lambda k: wuv[:, k, :], 2, D, dst=wuv)
    st = cm.next_stage()
    p.dma("sp", st[:, 0:256].rearrange("q (k m) -> q k m", k=KC), w_kr.rearrange("(k q) m -> q k m", q=128), writes=[st])
    stv = st[:, 0:256].rearrange("q (k m) -> q k m", k=KC)
    p.op("pool", lambda e: e.tensor_copy(wkr[:], stv), reads=[st], writes=[wkr])
    p.op("dve", lambda e: e.tensor_scalar(wsw[:, :, 0:16], stv[:, :, 16:32], -1.0, None, ALU.mult), reads=[st], writes=[wsw])
    p.op("dve", lambda e: e.tensor_copy(wsw[:, :, 16:32], stv[:, :, 0:16]), reads=[st], writes=[wsw])

    xblk = [p.sb("xblk%d" % i, [128, KC, BLK], F32) for i in range(2)]
    hT = p.sb("hT", [128, KC, BLK], BF16)
    latf = p.sb("latf", [128, 2, BLK], F32)
    ckv = p.sb("ckv", [128, 2, BLK], BF16)
    kTs = [p.sb("kTs%d" % i, [128, KC, BLK], BF16) for i in range(2)]
    krs = p.sb("krs", [32, BLK], BF16)
    t1 = p.sb("t1", [32, BLK], F32)
    t2 = p.sb("t2", [32, BLK], F32)
    vs = [p.sb("vs%d" % i, [128, D], BF16) for i in range(2)]
    psn = p.ps("psn", [128, 512], F32)
    pmm = [p.ps("pmm%d" % i, [128, 512], F32) for i in range(5)]
    pmi = [0]

    def next_ps():
        t = pmm[pmi[0] % 5]
        pmi[0] += 1
        return t

    for b in range(NBLK):
        c0 = b * BLK
        xb = xblk[b % 2]
        p.dma("sp", xb[:], xT[:, :, c0:c0 + BLK], writes=[xb])
        norm_block(p, cm, xb, KC, BLK, A, modt, lambda k: hT[:, k, :], hT, wk, float(D), psn)
        for m in range(2):
            ps = next_ps()
            for k in range(KC):
                p.op("pe", lambda e: e.matmul(ps[:], wdkv[:, k, m * 128:(m + 1) * 128], hT[:, k, :], start=(k == 0), stop=(k == KC - 1)),
                     reads=[wdkv, hT], writes=[ps])
            p.op("act", lambda e: e.activation(latf[:, m, :], ps[:], AF.Copy), reads=[ps], writes=[latf])
        norm_block(p, cm, latf, 2, BLK, lg, zer, lambda k: ckv[:, k, :], ckv, wk, 256.0, psn)
        ps1 = next_ps()
        ps2 = next_ps()
        for k in range(KC):
            p.op("pe", lambda e: e.matmul(ps1[0:32, :], wkr[:, k, :], hT[:, k, :], start=(k == 0), stop=(k == KC - 1)),
                 reads=[wkr, hT], writes=[ps1])
        for k in range(KC):
            p.op("pe", lambda e: e.matmul(ps2[0:32, :], wsw[:, k, :], hT[:, k, :], start=(k == 0), stop=(k == KC - 1)),
                 reads=[wsw, hT], writes=[ps2])
        p.op("dve", lambda e: e.tensor_tensor(t1[:], ps1[0:32, :], cos[:, c0:c0 + BLK], ALU.mult), reads=[ps1, cos], writes=[t1])
        p.op("dve", lambda e: e.tensor_tensor(t2[:], ps2[0:32, :], sin[:, c0:c0 + BLK], ALU.mult), reads=[ps2, sin], writes=[t2])
        p.op("dve", lambda e: e.tensor_tensor(krs[:], t1[:], t2[:], ALU.add), reads=[t1, t2], writes=[krs])
        p.dma("sp", kro[:, c0:c0 + BLK], krs[:], reads=[krs], is_output=True)
        kt = kTs[b % 2]
        for m in range(KC):
            ps = next_ps()
            for k in range(2):
                p.op("pe", lambda e: e.matmul(ps[:], wuk[:, k, m * 128:(m + 1) * 128], ckv[:, k, :], start=(k == 0), stop=(k == 1)),
                     reads=[wuk, ckv], writes=[ps])
            if m % 2 == 0:
                p.op("act", lambda e: e.activation(kt[:, m, :], ps[:], AF.Copy), reads=[ps], writes=[kt])
            else:
                p.op("dve", lambda e: e.tensor_copy(kt[:, m, :], ps[:]), reads=[ps], writes=[kt])
        p.dma("sp", kTo[:, :, c0:c0 + BLK], kt[:], reads=[kt], is_output=True)
        for tl in range(4):
            vt = vs[tl % 2]
            for half in range(2):
                ps = next_ps()
                for k in range(2):
                    p.op("pe", lambda e: e.matmul(ps[:], ckv[:, k, tl * 128:(tl + 1) * 128], wuv[:, k, half * 512:(half + 1) * 512],
                                                  start=(k == 0), stop=(k == 1)), reads=[ckv, wuv], writes=[ps])
                if half == 0:
                    p.op("act", lambda e: e.activation(vt[:, 0:512], ps[:], AF.Copy), reads=[ps], writes=[vt])
                else:
                    p.op("dve", lambda e: e.tensor_copy(vt[:, 512:1024], ps[:]), reads=[ps], writes=[vt])
            p.dma("sp", vo[:, b * 4 + tl, :], vt[:], reads=[vt], is_output=True)
    p.finish()
    return nc


ATTN_SCALE = 1.0 / (96.0 ** 0.5)


def build_q():
    nc = bass.Bass("TRN2", target_bir_lowering=False)
    xT = ext_in(nc, "xT", [128, KC, T])
    modd = ext_in(nc, "mod", [128, 24])
    ngd = ext_in(nc, "ng", [128, KC])
    w_dq = ext_in(nc, "w_dq", [D, 384])
    qgd = ext_in(nc, "qg", [128, 3])
    w_uq = ext_in(nc, "w_uq", [384, 1536])
    posd = ext_in(nc, "pos", [1, T], I32)
    freqd = ext_in(nc, "freq", [32, 1])
    qo = ext_out(nc, "q", [96, 16, T], BF16)

    p = Prog(nc)
    cm = Common(p)
    wk = norm_work(p)
    modt = small_in(p, "modt", modd, [128, 24])
    ng = small_in(p, "ngt", ngd, [128, KC])
    qg = small_in(p, "qgt", qgd, [128, 3])
    zer = p.sb("zer", [128, KC], F32)
    p.op("pool", lambda e: e.memset(zer[:], 0.0), writes=[zer])
    A = mod_cols(p, "mq", modt, ng, 0, 1)
    cos, sin = rope_tables(p, posd, freqd, T, ATTN_SCALE, "rq", pb=64)
    wdq = p.sb("wdq", [128, KC, 384], BF16)
    load_cast(p, cm, w_dq, lambda k: wdq[:, k, :], KC, 384, dst=wdq)
    wqh = p.sb("wqh", [128, 3, 16, 96], BF16)
    wsw = p.sb("wsw", [128, 3, 16, 96], BF16)
    p.op("pool", lambda e: e.memset(wsw[:], 0.0), writes=[wsw])
    for k in range(3):
        st = cm.next_stage()
        p.dma("sp", st[:, 0:1536], w_uq[k * 128:(k + 1) * 128, :], writes=[st])
        sv = st[:, 0:1536].rearrange("q (h c) -> q h c", h=16)
        p.op("pool", lambda e: e.tensor_copy(wqh[:, k, :, :], sv), reads=[st], writes=[wqh])
        p.op("dve", lambda e: e.tensor_scalar(wsw[:, k, :, 64:80], sv[:, :, 80:96], -1.0, None, ALU.mult), reads=[st], writes=[wsw])
        p.op("dve", lambda e: e.tensor_copy(wsw[:, k, :, 80:96], sv[:, :, 64:80]), reads=[st], writes=[wsw])

    xblk = [p.sb("xblk%d" % i, [128, KC, BLK], F32) for i in range(2)]
    hT = p.sb("hT", [128, KC, BLK], BF16)
    qlf = p.sb("qlf", [128, 3, BLK], F32)
    ql = p.sb("ql", [128, 3, BLK], BF16)
    qs = [p.sb("qs%d" % i, [96, 16, BLK], BF16) for i in range(2)]
    t1 = p.sb("t1", [96, BLK], F32)
    t2 = p.sb("t2", [96, BLK], F32)
    psn = p.ps("psn", [128, 512], F32)
    pmm = [p.ps("pmm%d" % i, [128, 512], F32) for i in range(6)]
    pmi = [0]

    def next_ps():
        t = pmm[pmi[0] % 6]
        pmi[0] += 1
        return t

    for b in range(NBLK):
        c0 = b * BLK
        xb = xblk[b % 2]
        p.dma("sp", xb[:], xT[:, :, c0:c0 + BLK], writes=[xb])
        norm_block(p, cm, xb, KC, BLK, A, modt, lambda k: hT[:, k, :], hT, wk, float(D), psn)
        for m in range(3):
            ps = next_ps()
            for k in range(KC):
                p.op("pe", lambda e: e.matmul(ps[:], wdq[:, k, m * 128:(m + 1) * 128], hT[:, k, :], start=(k == 0), stop=(k == KC - 1)),
                     reads=[wdq, hT], writes=[ps])
            p.op("act", lambda e: e.activation(qlf[:, m, :], ps[:], AF.Copy), reads=[ps], writes=[qlf])
        norm_block(p, cm, qlf, 3, BLK, qg, zer, lambda k: ql[:, k, :], ql, wk, 384.0, psn)
        q_s = qs[b % 2]
        for h in range(16):
            ps1 = next_ps()
            ps2 = next_ps()
            for k in range(3):
                p.op("pe", lambda e: e.matmul(ps1[0:96, :], wqh[:, k, h, :], ql[:, k, :], start=(k == 0), stop=(k == 2)),
                     reads=[wqh, ql], writes=[ps1])
            for k in range(3):
                p.op("pe", lambda e: e.matmul(ps2[0:96, :], wsw[:, k, h, :], ql[:, k, :], start=(k == 0), stop=(k == 2)),
                     reads=[wsw, ql], writes=[ps2])
            p.op("act", lambda e: e.activation(q_s[0:64, h, :], ps1[0:64, :], AF.Copy, scale=ATTN_SCALE), reads=[ps1], writes=[q_s])
            p.op("dve", lambda e: e.tensor_tensor(t1[64:96, :], ps1[64:96, :], cos[64:96, c0:c0 + BLK], ALU.mult), reads=[ps1, cos], writes=[t1])
            p.op("dve", lambda e: e.tensor_tensor(t2[64:96, :], ps2[64:96, :], sin[64:96, c0:c0 + BLK], ALU.mult), reads=[ps2, sin], writes=[t2])
            p.op("pool", lambda e: e.tensor_tensor(q_s[64:96, h, :], t1[64:96, :], t2[64:96, :], ALU.add), reads=[t1, t2], writes=[q_s])
        p.dma("sp", qo[:, :, c0:c0 + BLK], q_s[:], reads=[q_s], is_output=True)
    p.finish()
    return nc


HPC = 4
QB = 512
NQB = SEQ // QB
NKT = SEQ // 128


def build_att(nqb=NQB):
    nc = bass.Bass("TRN2", target_bir_lowering=False)
    qd = ext_in(nc, "q", [96, HPC, SEQ], BF16)
    kd = ext_in(nc, "k", [96, HPC, SEQ], BF16)
    vd = ext_in(nc, "v", [128, HPC, NKT, 64], BF16)
    od = ext_out(nc, "o", [64, HPC, SEQ], BF16)

    p = Prog(nc)
    onesf = p.sb("onesf", [128, 64], F32)
    p.op("pool", lambda e: e.memset(onesf[:], 1.0), writes=[onesf])
    masks = p.sb("masks", [128, 4, QB], BF16)
    mi = p.sb("mi", [128, QB], F32)
    for d in range(4):
        p.op("pool", lambda e: e.iota(mi[:], [[1, QB]], base=-128 * d, channel_multiplier=-1,
                                      allow_small_or_imprecise_dtypes=True), writes=[mi])
        p.op("dve", lambda e: e.tensor_scalar(masks[:, d, :], mi[:], 0.0, None, ALU.is_ge), reads=[mi], writes=[masks])
    qh = [p.sb("qh%d" % i, [96, SEQ], BF16) for i in range(2)]
    kh = [p.sb("kh%d" % i, [96, SEQ], BF16) for i in range(2)]
    vh = [p.sb("vh%d" % i, [128, NKT, 65], BF16) for i in range(2)]
    for i in range(2):
        p.op("pool", lambda e: e.memset(vh[i][:, :, 64:65], 1.0), writes=[vh[i]])
    PT = [p.sb("PT%d" % i, [128, QB], BF16) for i in range(4)]
    rden = p.sb("rden", [128, QB], F32)
    num = p.sb("num", [64, QB], F32)
    ob = [p.sb("ob%d" % i, [64, QB], BF16) for i in range(2)]
    sps = [p.ps("sps%d" % i, [128, QB], F32) for i in range(4)]
    accs = [p.ps("accs%d" % i, [128, QB], F32) for i in range(2)]
    pbc = p.ps("pbc", [64, QB], F32)

    it = 0
    for h in range(HPC):
        q_ = qh[h % 2]
        k_ = kh[h % 2]
        v_ = vh[h % 2]
        p.dma("sp", q_[:], qd[:, h, :], writes=[q_])
        p.dma("sp", k_[:], kd[:, h, :], writes=[k_])
        p.dma("sp", v_[:, :, 0:64], vd[:, h, :, :], writes=[v_])
        for qb in range(nqb):
            acc = accs[qb % 2]
            nkt = 4 * (qb + 1)
            for kt in range(nkt):
                ps = sps[it % 4]
                pt = PT[it % 4]
                it += 1
                p.op("pe", lambda e: e.matmul(ps[:], k_[:, kt * 128:(kt + 1) * 128], q_[:, qb * QB:(qb + 1) * QB], start=True, stop=True),
                     reads=[k_, q_], writes=[ps])
                p.op("act", lambda e: e.activation(pt[:], ps[:], AF.Exp), reads=[ps], writes=[pt])
                d = kt - 4 * qb
                if d >= 0:
                    p.op("dve", lambda e: e.tensor_tensor(pt[:], pt[:], masks[:, d, :], ALU.mult), reads=[pt, masks], writes=[pt])
                p.op("pe", lambda e: e.matmul(acc[0:65, :], v_[:, kt, :], pt[:], start=(kt == 0), stop=(kt == nkt - 1)),
                     reads=[v_, pt], writes=[acc])
            p.op("dve", lambda e: e.reciprocal(rden[64:65, :], acc[64:65, :]), reads=[acc], writes=[rden])
            p.op("pe", lambda e: e.matmul(pbc[:], onesf[64:65, 0:64], rden[64:65, :], start=True, stop=True),
                 reads=[onesf, rden], writes=[pbc])
            p.op("act", lambda e: e.activation(num[:], acc[0:64, :], AF.Copy), reads=[acc], writes=[num])
            o_ = ob[qb % 2]
            p.op("dve", lambda e: e.tensor_tensor(o_[:], num[:], pbc[:], ALU.mult), reads=[num, pbc], writes=[o_])
            p.dma("sp", od[:, h, qb * QB:(qb + 1) * QB], o_[:], reads=[o_], is_output=True)
    p.finish()
    print("att nins", p.nins)
    return nc


def build_o():
    nc = bass.Bass("TRN2", target_bir_lowering=False)
    xT = ext_in(nc, "xT", [128, KC, T])
    aT = ext_in(nc, "aT", [128, KC, T], BF16)
    modd = ext_in(nc, "mod", [128, 24])
    w_o = ext_in(nc, "w_o", [D, D])
    xo = ext_out(nc, "xo", [128, KC, T])
    p = Prog(nc)
    cm = Common(p)
    modt = small_in(p, "modt", modd, [128, 24])
    wo = p.sb("wo", [128, KC, D], BF16)
    load_cast(p, cm, w_o, lambda k: wo[:, k, :], KC, D, dst=wo)
    xblk = [p.sb("xblk%d" % i, [128, KC, BLK], F32) for i in range(2)]
    ablk = [p.sb("ablk%d" % i, [128, KC, BLK], BF16) for i in range(2)]
    pmm = [p.ps("pmm%d" % i, [128, 512], F32) for i in range(4)]
    for b in range(NBLK):
        c0 = b * BLK
        xb = xblk[b % 2]
        ab = ablk[b % 2]
        p.dma("sp", xb[:], xT[:, :, c0:c0 + BLK], writes=[xb])
        p.dma("sp", ab[:], aT[:, :, c0:c0 + BLK], writes=[ab])
        for m in range(KC):
            ps = pmm[m % 4]
            for k in range(KC):
                p.op("pe", lambda e: e.matmul(ps[:], wo[:, k, m * 128:(m + 1) * 128], ab[:, k, :], start=(k == 0), stop=(k == KC - 1)),
                     reads=[wo, ab], writes=[ps])
            p.op("dve", lambda e: e.scalar_tensor_tensor(xb[:, m, :], ps[:], modt[:, 16 + m:17 + m], xb[:, m, :], ALU.mult, ALU.add),
                 reads=[ps, modt, xb], writes=[xb])
        p.dma("sp", xo[:, :, c0:c0 + BLK], xb[:], reads=[xb], is_output=True)
    p.finish()
    return nc


def build_fin():
    nc = bass.Bass("TRN2", target_bir_lowering=False)
    xT = ext_in(nc, "xT", [128, KC, T])
    gd = ext_in(nc, "g", [128, KC])
    xo = ext_out(nc, "xo", [128, KC, T])
    p = Prog(nc)
    cm = Common(p, need_stage=False)
    wk = norm_work(p)
    g = small_in(p, "gt", gd, [128, KC])
    zer = p.sb("zer", [128, KC], F32)
    p.op("pool", lambda e: e.memset(zer[:], 0.0), writes=[zer])
    xblk = [p.sb("xblk%d" % i, [128, KC, BLK], F32) for i in range(2)]
    oblk = [p.sb("oblk%d" % i, [128, KC, BLK], F32) for i in range(2)]
    psn = p.ps("psn", [128, 512], F32)
    for b in range(NBLK):
        c0 = b * BLK
        xb = xblk[b % 2]
        o_ = oblk[b % 2]
        p.dma("sp", xb[:], xT[:, :, c0:c0 + BLK], writes=[xb])
        norm_block(p, cm, xb, KC, BLK, g, zer, lambda k: o_[:, k, :], o_, wk, float(D), psn)
        p.dma("sp", xo[:, :, c0:c0 + BLK], o_[:], reads=[o_], is_output=True)
    p.finish()
    return nc


NCORES = 8


import os
import time as _time


def _run(nc, maps):
    t0 = _time.time()
    res = run_bass_kernel_spmd(nc, maps, core_ids=list(range(NCORES)))
    if os.environ.get("KDBG_DIR"):
        print("launch done in %.1fs" % (_time.time() - t0), flush=True)
    return res.results


def _dbg(name, arrs):
    d = os.environ.get("KDBG_DIR")
    if d:
        np.save(os.path.join(d, name + ".npy"), np.stack([np.asarray(a).astype(np.float32) for a in arrs]))


def kernel(**inputs):
    inp = {k: np.asarray(v) for k, v in inputs.items()}
    x = inp["x"].astype(np.float32, copy=False)
    c = inp["c"].astype(np.float32, copy=False)
    pos = inp["positions"].astype(np.int32, copy=False)
    depth = inp["ada_w"].shape[0]
    n_a = inp["lru_w_in"].shape[0]

    maps = []
    for r in range(NCORES):
        b, l = r // 4, r % 4
        maps.append({"W": np.ascontiguousarray(inp["ada_w"][l]), "cT": colT(c[b]),
                     "bT": np.ascontiguousarray(inp["ada_b"][l].reshape(48, 128).T)})
    res = _run(build_mod(), maps)
    mods = [[res[b * 4 + l]["out"] for l in range(depth)] for b in range(2)]
    wkv = np.zeros((D, 6144), np.float32)
    wkv[:, :2048] = inp["kv_ada_w"]
    bkv = np.zeros((6144,), np.float32)
    bkv[:2048] = inp["kv_ada_b"]
    maps = [{"W": wkv, "cT": colT(c[r // 4]), "bT": np.ascontiguousarray(bkv.reshape(48, 128).T)} for r in range(NCORES)]
    res = _run(build_mod(), maps)
    modkv = [np.ascontiguousarray(res[b * 4]["out"][:, 0:16]) for b in range(2)]

    maps = []
    for r in range(NCORES):
        maps.append({"U": np.ascontiguousarray(inp["peer_u"][:, r * 2048:(r + 1) * 2048, :].reshape(depth * 2048, D)),
                     "V": np.ascontiguousarray(inp["peer_v"][:, r * 2048:(r + 1) * 2048, :].reshape(depth * 2048, D))})
    res = _run(build_conv(), maps)
    uT_l = [np.concatenate([res[r]["uT"][:, :, l * 2048:(l + 1) * 2048] for r in range(NCORES)], axis=2) for l in range(depth)]
    vb_l = [np.concatenate([res[r]["vb"][l * 2048:(l + 1) * 2048] for r in range(NCORES)], axis=0) for l in range(depth)]
    del res

    freq = (10000.0 ** (-np.arange(16, dtype=np.float32) / 16.0)).astype(np.float32)
    freq32 = np.ascontiguousarray(np.concatenate([freq, freq]).reshape(32, 1))

    xfm = [to_fm(x[r // 4, (r % 4) * T:(r % 4 + 1) * T]) for r in range(NCORES)]
    posc = [np.ascontiguousarray(pos[r // 4, (r % 4) * T:(r % 4 + 1) * T].reshape(1, T)) for r in range(NCORES)]

    kv = None
    for l in range(depth):
        if l == n_a:
            maps = []
            for r in range(NCORES):
                b = r // 4
                maps.append({"xT": xfm[r], "mod": modkv[b], "ng": colT(inp["kv_norm_g"]),
                             "w_dkv": np.ascontiguousarray(inp["mla_w_dkv"]), "w_kr": np.ascontiguousarray(inp["mla_w_kr"]),
                             "lg": colT(inp["mla_kv_latent_g"]), "w_uk": np.ascontiguousarray(inp["mla_w_uk"]),
                             "w_uv": np.ascontiguousarray(inp["mla_w_uv"]), "pos": posc[r], "freq": freq32})
            res = _run(build_kv(), maps)
            kv = []
            for b in range(2):
                kn = np.concatenate([res[b * 4 + cc]["kT"] for cc in range(4)], axis=2)
                kn = kn.transpose(1, 0, 2).reshape(D, SEQ)
                kr = np.concatenate([res[b * 4 + cc]["krT"] for cc in range(4)], axis=1)
                vv = np.concatenate([res[b * 4 + cc]["v"] for cc in range(4)], axis=1)
                kv.append((kn, kr, vv))
            del res
            _dbg("kn", [kv[0][0][:, 0:1024], kv[1][0][:, 0:1024]])
            _dbg("kr", [kv[0][1], kv[1][1]])
            _dbg("vv", [kv[0][2][:, 0:8], kv[1][2][:, 0:8]])
        if l < n_a:
            zs = [(np.zeros((128, KC), np.float32),) * 2] * NCORES
            maps = [lru_inputs(inp, l, r, xfm, mods[r // 4][l], zs, False) for r in range(NCORES)]
            res = _run(build_lru(), maps)
            sums = [(res[r]["sA"], res[r]["sB"]) for r in range(NCORES)]
            maps = [lru_inputs(inp, l, r, xfm, mods[r // 4][l], sums, True) for r in range(NCORES)]
            res = _run(build_lru(), maps)
            xfm = [res[r]["xo"] for r in range(NCORES)]
            _dbg("xa%d" % l, xfm)
        else:
            j = l - n_a
            maps = []
            for r in range(NCORES):
                maps.append({"xT": xfm[r], "mod": np.ascontiguousarray(mods[r // 4][l][:, 0:24]), "ng": colT(inp["norm_mix_g"][l]),
                             "w_dq": np.ascontiguousarray(inp["mla_w_dq"][j]), "qg": colT(inp["mla_q_latent_g"][j]),
                             "w_uq": np.ascontiguousarray(inp["mla_w_uq"][j]), "pos": posc[r], "freq": freq32})
            res = _run(build_q(), maps)
            maps = []
            for r in range(NCORES):
                b, hg = r // 4, r % 4
                qf = np.concatenate([res[b * 4 + cc]["q"] for cc in range(4)], axis=2)
                kn, kr, vv = kv[b]
                kk = np.empty((96, HPC, SEQ), dtype=kn.dtype)
                for hh in range(HPC):
                    h = hg * HPC + hh
                    kk[0:64, hh, :] = kn[h * 64:(h + 1) * 64, :]
                    kk[64:96, hh, :] = kr
                vsel = vv[:, :, hg * 256:(hg + 1) * 256].reshape(128, NKT, HPC, 64).transpose(0, 2, 1, 3)
                maps.append({"q": np.ascontiguousarray(qf[:, hg * HPC:(hg + 1) * HPC, :]), "k": kk,
                             "v": np.ascontiguousarray(vsel)})
            res = _run(build_att(), maps)
            maps = []
            for r in range(NCORES):
                b, cc = r // 4, r % 4
                parts = [res[b * 4 + hg]["o"][:, :, cc * T:(cc + 1) * T] for hg in range(4)]
                at = np.concatenate([pp.transpose(1, 0, 2).reshape(HPC * 64, T) for pp in parts], axis=0)
                aT = np.ascontiguousarray(at.reshape(KC, 128, T).transpose(1, 0, 2))
                maps.append({"xT": xfm[r], "aT": aT, "mod": np.ascontiguousarray(mods[b][l][:, 0:24]),
                             "w_o": np.ascontiguousarray(inp["mla_w_o"][j])})
            _dbg("att%d" % l, [res[r]["o"][:, :, 0:1024] for r in range(NCORES)])
            res = _run(build_o(), maps)
            xfm = [res[r]["xo"] for r in range(NCORES)]
            _dbg("xa%d" % l, xfm)
        maps = [peer_inputs(inp, l, xfm[r], mods[r // 4][l], uT_l[l], vb_l[l]) for r in range(NCORES)]
        res = _run(build_peer(), maps)
        xfm = [res[r]["xo"] for r in range(NCORES)]
        _dbg("xb%d" % l, xfm)
    maps = [{"xT": xfm[r], "g": colT(inp["final_g"])} for r in range(NCORES)]
    res = _run(build_fin(), maps)
    out = np.empty((2, SEQ, D), np.float32)
    for r in range(NCORES):
        out[r // 4, (r % 4) * T:(r % 4 + 1) * T, :] = from_fm(res[r]["xo"])
    return out
```
